# Optimizing a Trainium2 kernel written in Bass

```python
import jax, jax.numpy as jnp
from jax import lax
import numpy as np

D_MODEL = 1024
BATCH = 8
SEQ = 8192
DEPTH = 2

CHUNK = 64
BAND_CHUNKS = 9
ATTN_HEADS = 8
HEAD_DIM = 64
ATTN_WIDTH = ATTN_HEADS * HEAD_DIM
POOL_WINDOWS = (2, 4, 8, 16)
POOL_GROUPS = len(POOL_WINDOWS)
POOL_WIDTH = D_MODEL // 2
POOL_GROUP_DIM = POOL_WIDTH // POOL_GROUPS
MAX_REL_DIST = 256
N_REL = 2 * MAX_REL_DIST + 1
D_FF = 2816
CONV_WIDTH = 3
N_BRANCH = 2
IN_WIDTH = 3 * ATTN_WIDTH + POOL_WIDTH + N_BRANCH * D_MODEL
EPS = 1e-6

kernel_name = "hybrid_chunk_attn_pool_sandwich"


def rms_norm(x, g):
    xf = x.astype(jnp.float32)
    y = xf * lax.rsqrt(jnp.mean(xf * xf, axis=-1, keepdims=True) + EPS)
    return (y * g.astype(jnp.float32)).astype(x.dtype)


def chunk_band_attention(q, k, v, rel_bias):
    b, s, h, dh = q.shape
    n_chunks = s // CHUNK
    band = BAND_CHUNKS * CHUNK
    lead = (BAND_CHUNKS - 1) * CHUNK
    pad = ((0, 0), (lead, 0), (0, 0), (0, 0))
    k_pad = jnp.pad(k, pad)
    v_pad = jnp.pad(v, pad)
    dist = jnp.arange(CHUNK)[:, None] + lead - jnp.arange(band)[None, :]
    idx = jnp.clip(dist, -MAX_REL_DIST, MAX_REL_DIST) + MAX_REL_DIST
    bias = rel_bias.astype(jnp.float32)[:, idx]
    scale = HEAD_DIM ** -0.5
    key_offsets = jnp.arange(band)

    def one_chunk(c):
        start = c * CHUNK
        q_c = lax.dynamic_slice_in_dim(q, start, CHUNK, axis=1)
        k_c = lax.dynamic_slice_in_dim(k_pad, start, band, axis=1)
        v_c = lax.dynamic_slice_in_dim(v_pad, start, band, axis=1)
        sc = jnp.einsum('bqhd,bkhd->bhqk', q_c, k_c,
                        preferred_element_type=jnp.float32) * scale + bias
        valid = (start - lead + key_offsets) >= 0
        sc = jnp.where(valid[None, None, None, :], sc, -1e30)
        p = jax.nn.softmax(sc, axis=-1).astype(v.dtype)
        return jnp.einsum('bhqk,bkhd->bqhd', p, v_c)

    out = lax.map(one_chunk, jnp.arange(n_chunks))
    return jnp.moveaxis(out, 0, 1).reshape(b, s, h * dh)


def multiscale_pool(u, w_group, scale):
    b, s, c = u.shape
    uf = u.astype(jnp.float32)
    max_w = max(POOL_WINDOWS)
    cs = jnp.pad(jnp.cumsum(uf, axis=1), ((0, 0), (max_w, 0), (0, 0)))
    t = jnp.arange(s)
    outs = []
    for g, w in enumerate(POOL_WINDOWS):
        sl = slice(g * POOL_GROUP_DIM, (g + 1) * POOL_GROUP_DIM)
        win = cs[:, max_w:, sl] - cs[:, max_w - w:max_w - w + s, sl]
        cnt = jnp.minimum(t + 1, w).astype(jnp.float32)[None, :, None]
        outs.append(win / cnt - uf[:, :, sl])
    pooled = jnp.stack(outs, axis=2).astype(u.dtype)
    mixed = jnp.einsum('bsgc,gcd->bsgd', pooled, w_group).reshape(b, s, c)
    return mixed * scale


def conv_gated_ffn(x, w_up, conv_w, conv_b, w_down):
    hu = x @ w_up
    s = hu.shape[1]
    hp = jnp.pad(hu, ((0, 0), (CONV_WIDTH - 1, 0), (0, 0)))
    hc = conv_b + conv_w[CONV_WIDTH - 1] * hu
    for i in range(CONV_WIDTH - 1):
        hc = hc + conv_w[i] * hp[:, i:i + s]
    val, gate = jnp.split(hc, 2, axis=-1)
    return (jax.nn.gelu(gate, approximate=True) * val) @ w_down


def setup_inputs(seed: int = 0) -> dict:
    key = jax.random.key(seed)
    ks = jax.random.split(key, 20)
    f32 = jnp.float32

    def nrm(k, shape, s):
        return jax.random.normal(k, shape, f32) * s

    return {
        "x": jax.random.normal(ks[0], (BATCH, SEQ, D_MODEL), f32),
        "norm_mix_pre": 1.0 + nrm(ks[1], (DEPTH, D_MODEL), 0.05),
        "w_in": nrm(ks[2], (DEPTH, D_MODEL, IN_WIDTH), D_MODEL ** -0.5),
        "b_gate": nrm(ks[3], (DEPTH, N_BRANCH * D_MODEL), 0.01),
        "rel_bias": nrm(ks[4], (DEPTH, ATTN_HEADS, N_REL), 0.1),
        "w_attn_out": nrm(ks[5], (DEPTH, ATTN_WIDTH, D_MODEL), ATTN_WIDTH ** -0.5),
        "w_pool_group": nrm(ks[6], (DEPTH, POOL_GROUPS, POOL_GROUP_DIM, POOL_GROUP_DIM), POOL_GROUP_DIM ** -0.5),
        "pool_scale": 1.0 + nrm(ks[7], (DEPTH, POOL_WIDTH), 0.1),
        "w_pool_out": nrm(ks[8], (DEPTH, POOL_WIDTH, D_MODEL), POOL_WIDTH ** -0.5),
        "w_o": nrm(ks[9], (DEPTH, D_MODEL, D_MODEL), D_MODEL ** -0.5),
        "norm_mix_post": 1.0 + nrm(ks[10], (DEPTH, D_MODEL), 0.05),
        "norm_ffn_pre": 1.0 + nrm(ks[11], (DEPTH, D_MODEL), 0.05),
        "w_up": nrm(ks[12], (DEPTH, D_MODEL, 2 * D_FF), D_MODEL ** -0.5),
        "conv_w": nrm(ks[13], (DEPTH, CONV_WIDTH, 2 * D_FF), CONV_WIDTH ** -0.5),
        "conv_b": nrm(ks[14], (DEPTH, 2 * D_FF), 0.01),
        "w_down": nrm(ks[15], (DEPTH, D_FF, D_MODEL), D_FF ** -0.5),
        "norm_ffn_post": 1.0 + nrm(ks[16], (DEPTH, D_MODEL), 0.05),
    }


def reference(x, norm_mix_pre, w_in, b_gate, rel_bias, w_attn_out, w_pool_group, pool_scale,
              w_pool_out, w_o, norm_mix_post, norm_ffn_pre, w_up, conv_w, conv_b, w_down,
              norm_ffn_post):
    b, s, _ = x.shape
    splits = [ATTN_WIDTH, 2 * ATTN_WIDTH, 3 * ATTN_WIDTH, 3 * ATTN_WIDTH + POOL_WIDTH]
    for l in range(DEPTH):
        h = rms_norm(x, norm_mix_pre[l])
        proj = h @ w_in[l]
        q, k, v, u, gates = jnp.split(proj, splits, axis=-1)
        q = q.reshape(b, s, ATTN_HEADS, HEAD_DIM)
        k = k.reshape(b, s, ATTN_HEADS, HEAD_DIM)
        v = v.reshape(b, s, ATTN_HEADS, HEAD_DIM)
        y_a = chunk_band_attention(q, k, v, rel_bias[l]) @ w_attn_out[l]
        y_b = multiscale_pool(u, w_pool_group[l], pool_scale[l]) @ w_pool_out[l]
        g_a, g_b = jnp.split(jax.nn.sigmoid(gates + b_gate[l]), N_BRANCH, axis=-1)
        mix = (g_a * y_a + g_b * y_b) @ w_o[l]
        x = x + rms_norm(mix, norm_mix_post[l])
        f = conv_gated_ffn(rms_norm(x, norm_ffn_pre[l]), w_up[l], conv_w[l], conv_b[l], w_down[l])
        x = x + rms_norm(f, norm_ffn_post[l])
    return x
```

```python
import numpy as np
from contextlib import ExitStack
import concourse.bass as bass
import concourse.mybir as mybir
from concourse.bass_utils import run_bass_kernel_spmd

F32 = mybir.dt.float32
BF16 = mybir.dt.bfloat16
AF = mybir.ActivationFunctionType
ALU = mybir.AluOpType

ENGS = ("pe", "act", "dve", "pool", "sp")

SEQ = 8192
D = 1024
T = 512
TB = 4
NCH = 8
DFF = 2816
NFF = 22
EPS = 1e-6
NSLOT = 4
SLOTC = 4096
GROUPS = []
_off = 0
for _nm, _nc in ([("q", 4096), ("k", 4096), ("v", 4096), ("wpg", 512), ("u", 4096)]
                 + [("e%d" % i, 3072) for i in range(8)] + [("wo0", 4096), ("wo1", 4096)]
                 + [("up%d" % i, 4096) for i in range(11)] + [("dn%d" % i, 2816) for i in range(8)]):
    GROUPS.append((_nm, _off, _nc))
    _off += _nc
WCOLS = _off

SM = {}
_o = 0
for _nm, _n in [("g_mpre", 16), ("g_mpost", 16), ("g_fpre", 16), ("g_fpost", 16), ("b_gate", 32),
                ("pscale", 8), ("convw", 2 * 3 * 44), ("convb", 2 * 44), ("invc", 64)]:
    SM[_nm] = _o
    _o += _n
NSM = _o


class V:
    __slots__ = ("ap", "keys")

    def __init__(self, ap, *keys):
        self.ap = ap
        self.keys = tuple(keys)


class Prog:
    def __init__(self, nc):
        self.nc = nc
        self.streams = {e: [] for e in ENGS}
        self.cnt = {e: 0 for e in ENGS}
        self.seen = {e: {} for e in ENGS}
        self.last_w = {}
        self.readers = {}
        self.sems = {}
        self.dma_cnt = {}
        self.n_wait = 0
        self.n_ins = 0

    def _deps(self, eng, reads, writes, same_ok):
        deps = {}

        def need(tok):
            if tok[2] == eng and same_ok:
                return
            if tok[1] > deps.get(tok[0], 0):
                deps[tok[0]] = tok[1]
        for r in reads:
            lw = self.last_w.get(r)
            if lw is not None:
                need(lw)
        for w in writes:
            lw = self.last_w.get(w)
            if lw is not None:
                need(lw)
            for rd in self.readers.get(w, ()):
                need(rd)
        out = []
        seen = self.seen[eng]
        for k, v in deps.items():
            if seen.get(k, 0) < v:
                seen[k] = v
                out.append((k, v))
        return out

    def _record(self, reads, writes, tok):
        for r in reads:
            lst = self.readers.setdefault(r, [])
            lst[:] = [x for x in lst if x[0] != tok[0]]
            lst.append(tok)
        for w in writes:
            self.last_w[w] = tok
            self.readers[w] = []

    def op(self, eng, fn, reads=(), writes=(), same_ok=False):
        waits = self._deps(eng, reads, writes, same_ok)
        self.cnt[eng] += 1
        tok = (("e", eng), self.cnt[eng], eng)
        self._record(reads, writes, tok)
        self.n_wait += len(waits)
        self.n_ins += 1

        def emit(e, waits=waits, fn=fn, eng=eng):
            for k, v in waits:
                e.wait_ge(self.sems[k], v)
            fn(e).then_inc(self.sems[("e", eng)], 1)
        self.streams[eng].append(emit)

    def dma(self, queue, out, in_, semname, reads=(), writes=(), **kw):
        waits = self._deps(queue, reads, writes, False)
        k = ("d", semname)
        self.dma_cnt[k] = self.dma_cnt.get(k, 0) + 16
        tok = (k, self.dma_cnt[k], "dma")
        self._record(reads, writes, tok)
        self.n_wait += len(waits)
        self.n_ins += 1

        def emit(e, waits=waits, k=k):
            for kk, v in waits:
                e.wait_ge(self.sems[kk], v)
            e.dma_start(out=out, in_=in_, **kw).then_inc(self.sems[k], 16)
        self.streams[queue].append(emit)

    def barrier(self):
        targets = [(("e", e), self.cnt[e]) for e in ENGS if self.cnt[e] > 0]
        targets += [(k, v) for k, v in self.dma_cnt.items()]
        for eng in ENGS:
            waits = []
            seen = self.seen[eng]
            for k, v in targets:
                if k == ("e", eng):
                    continue
                if seen.get(k, 0) < v:
                    seen[k] = v
                    waits.append((k, v))

            def emit(e, waits=waits):
                for kk, v in waits:
                    e.wait_ge(self.sems[kk], v)
            self.streams[eng].append(emit)
        self.last_w.clear()
        self.readers.clear()

    @staticmethod
    def _k(*vs):
        ks = []
        for v in vs:
            if isinstance(v, V):
                ks.extend(v.keys)
        return ks

    @staticmethod
    def _a(v):
        return v.ap if isinstance(v, V) else v

    def mm(self, out, lhsT, rhs, start, stop):
        a = self._a
        self.op("pe", lambda e: e.matmul(a(out), a(lhsT), a(rhs), start=start, stop=stop),
                reads=self._k(lhsT, rhs), writes=self._k(out), same_ok=True)

    def tr(self, out, in_, ident):
        a = self._a
        self.op("pe", lambda e: e.transpose(a(out), a(in_), a(ident)),
                reads=self._k(in_, ident), writes=self._k(out), same_ok=True)

    def act(self, out, in_, func, bias=0.0, scale=1.0):
        a = self._a
        self.op("act", lambda e: e.activation(a(out), a(in_), func, bias=a(bias), scale=a(scale)),
                reads=self._k(in_, bias, scale), writes=self._k(out))

    def ts(self, eng, out, in0, s1, s2, op0, op1=None):
        a = self._a
        if op1 is None:
            f = lambda e: e.tensor_scalar(a(out), a(in0), a(s1), None, op0)
        else:
            f = lambda e: e.tensor_scalar(a(out), a(in0), a(s1), a(s2), op0, op1)
        self.op(eng, f, reads=self._k(in0, s1, s2), writes=self._k(out))

    def stt(self, out, in0, s, in1, op0, op1):
        a = self._a
        self.op("dve", lambda e: e.scalar_tensor_tensor(a(out), a(in0), a(s), a(in1), op0, op1),
                reads=self._k(in0, s, in1), writes=self._k(out))

    def tt(self, eng, out, in0, in1, op):
        a = self._a
        self.op(eng, lambda e: e.tensor_tensor(a(out), a(in0), a(in1), op),
                reads=self._k(in0, in1), writes=self._k(out))

    def cp(self, eng, out, in_):
        a = self._a
        if eng == "act":
            self.op(eng, lambda e: e.activation(a(out), a(in_), AF.Identity),
                    reads=self._k(in_), writes=self._k(out))
        else:
            self.op(eng, lambda e: e.tensor_copy(a(out), a(in_)), reads=self._k(in_), writes=self._k(out))

    def memset(self, eng, out, val):
        a = self._a
        self.op(eng, lambda e: e.memset(a(out), val), writes=self._k(out))

    def recip(self, out, in_):
        a = self._a
        self.op("dve", lambda e: e.reciprocal(a(out), a(in_)), reads=self._k(in_), writes=self._k(out))

    def build(self, stack):
        nc = self.nc
        for e in ENGS:
            self.sems[("e", e)] = stack.enter_context(nc.semaphore("s_" + e))
        for k in self.dma_cnt:
            self.sems[k] = stack.enter_context(nc.semaphore("d_" + str(k[1])))
        block = stack.enter_context(nc.Block())
        S = self.streams

        @block.tensor
        def _(e):
            for f in S["pe"]:
                f(e)

        @block.scalar
        def _(e):
            for f in S["act"]:
                f(e)

        @block.vector
        def _(e):
            for f in S["dve"]:
                f(e)

        @block.gpsimd
        def _(e):
            for f in S["pool"]:
                f(e)

        @block.sync
        def _(e):
            for f in S["sp"]:
                f(e)


def build_program(NT=16, layers=(0, 1)):
    nc = bass.Bass("TRN2", target_bir_lowering=False)
    dt_in = lambda name, shape, dt=F32: nc.dram_tensor(name, shape, dt, kind="ExternalInput").ap()
    x_d = dt_in("x", [SEQ, D])
    w_in_d = dt_in("w_in", [2, D, 4096])
    w_ao_d = dt_in("w_attn_out", [2, 512, D])
    w_pg_d = dt_in("w_pool_group", [2, 4, 128, 128])
    w_po_d = dt_in("w_pool_out", [2, 512, D])
    w_o_d = dt_in("w_o", [2, D, D])
    w_up_d = dt_in("w_up", [2, D, 2 * DFF])
    w_dn_d = dt_in("w_down", [2, DFF, D])
    btab_d = dt_in("btab", [2, 128, 5120])
    maskc_d = dt_in("maskc", [128, 5120])
    smalls_d = dt_in("smalls", [128, NSM])
    out_d = nc.dram_tensor("out", [SEQ, D], F32, kind="ExternalOutput").ap()
    wscr = nc.dram_tensor("wscr", [2, 128, WCOLS], BF16, kind="Internal").ap()
    bscr = nc.dram_tensor("bscr", [2, 128, 5120], BF16, kind="Internal").ap()

    P = Prog(nc)
    with ExitStack() as st:
        sb = lambda name, shape, dt: st.enter_context(nc.sbuf_tensor(name, shape, dt))
        wslot = [sb("wslot%d" % i, [128, SLOTC], BF16) for i in range(NSLOT)]
        xT = sb("xT", [128, NCH, T], F32)
        xio = [sb("xio%d" % i, [128, D], F32) for i in range(2)]
        xo = [sb("xo%d" % i, [128, D], F32) for i in range(2)]
        h = sb("h", [128, NCH, T], BF16)
        sq = [sb("sq%d" % i, [128, T], BF16) for i in range(2)]
        kT = {l: sb("kT%d" % l, [128, 4, 2 * T], BF16) for l in layers}
        Vb = {l: sb("Vb%d" % l, [128, 8, 8, 65], BF16) for l in layers}
        U = [sb("U%d" % i, [128, 16 + T], F32) for i in range(2)]
        uhalo = {l: sb("uhalo%d" % l, [128, 4, 16], F32) for l in layers}
        sAB = [sb("sAB%d" % i, [128, 16 + T], F32) for i in range(2)]
        R1 = sb("R1", [128, 16 * 512], BF16)
        R2 = sb("R2", [128, 22 * 512], BF16)
        R3 = sb("R3", [128, 16 * 512], BF16)
        tmpn = [sb("tmpn%d" % i, [128, T], F32) for i in range(2)]
        rr = sb("rr", [128, T], F32)
        chalo = {l: sb("chalo%d" % l, [128, 44, 2], F32) for l in layers}
        biasT = sb("biasT", [128, 8, 5, 128], BF16)
        smalls = sb("smalls_sb", [128, NSM], F32)
        identf = sb("identf", [128, 128], F32)
        identb = sb("identb", [128, 128], BF16)
        onesb = sb("onesb", [128, 128], BF16)
        rden = sb("rden", [128, 8], F32)
        t16 = sb("t16", [128, 16], F32)
        ps = st.enter_context(nc.psum_tensor("ps", [128, 8, 512], F32))

        def r_bf(R, name, s):
            return V(R[:, s * 512:(s + 1) * 512], (name, s))

        def r_f32(R, name, s):
            return V(R[:, s * 512:(s + 2) * 512].bitcast(F32), (name, s), (name, s + 1))

        def sm(name, col):
            c = SM[name] + col
            return V(smalls[:, c:c + 1], "smalls")

        bank_ctr = [0]

        def bank(n=7):
            b = bank_ctr[0] % n
            bank_ctr[0] += 1
            return b

        def PS(b, lo=0, hi=512):
            return V(ps[:, b, lo:hi], ("ps", b))

        P.dma("sp", smalls[:], smalls_d, "smld", writes=["smalls"])
        P.memset("pool", V(identf[:], "identf"), 1.0)
        P.op("pool", lambda e: e.affine_select(identf[:], identf[:], [[-1, 128]], ALU.is_equal, 0.0,
                                               base=0, channel_multiplier=1),
             reads=["identf"], writes=["identf"])
        P.cp("dve", V(identb[:], "identb"), V(identf[:], "identf"))
        P.memset("dve", V(onesb[:], "onesb"), 1.0 / 1024.0)
        for l in layers:
            P.memset("pool", V(kT[l][:], ("kT", l)), 0.0)
            P.memset("pool", V(Vb[l][:], ("Vb", l)), 0.0)
            P.memset("pool", V(Vb[l][:, :, :, 64:65], ("Vb", l)), 1.0)
            P.memset("pool", V(uhalo[l][:], ("uhalo", l)), 0.0)
            P.memset("pool", V(chalo[l][:], *[("chalo", l, m) for m in range(44)]), 0.0)
        stage_f = R2[:, 0:10240].bitcast(F32)
        stage_m = R1[:, 0:8192].bitcast(F32)
        r2keys = [("R2", s) for s in range(22)]
        r1keys = [("R1", s) for s in range(16)]
        r3keys = [("R3", s) for s in range(16)]
        stage_m2 = R3[:, 0:2048].bitcast(F32)
        P.dma("sp", stage_m, maskc_d[:, 0:4096], "mkld", writes=r1keys)
        P.dma("sp", stage_m2, maskc_d[:, 4096:5120], "mkld2", writes=r3keys)
        for l in layers:
            P.dma("sp", stage_f, btab_d[l], "btld", writes=r2keys)
            bt = biasT[:].rearrange("p a b c -> p (a b c)")
            P.stt(V(bt[:, 0:4096], "biasT"), V(stage_f[:, 0:4096], *r2keys), 8.0, V(stage_m, *r1keys),
                  ALU.mult, ALU.add)
            P.stt(V(bt[:, 4096:5120], "biasT"), V(stage_f[:, 4096:5120], *r2keys), 8.0, V(stage_m2, *r3keys),
                  ALU.mult, ALU.add)
            P.dma("sp", bscr[l], bt, "btst", reads=["biasT"], writes=[("bscr", l)])

        prep_i = [0]

        def prep(l, gname, parts):
            s = prep_i[0] % NSLOT
            prep_i[0] += 1
            gi = [g[0] for g in GROUPS].index(gname)
            _, off, ncols = GROUPS[gi]
            for dstf, src in parts:
                P.dma("pool", dstf(wslot[s]), src, "prep%d" % s, writes=[("wslot", s)])
            P.dma("sp", wscr[l, :, off:off + ncols], wslot[s][:, 0:ncols], "prepo%d" % s,
                  reads=[("wslot", s)], writes=[("wscr", l)])

        def v4(ap, c, kc):
            return ap.rearrange("p (c kc j) -> p c kc j", c=c, kc=kc)

        def ck(lo, kc):
            return lambda s: s[:, lo:lo + kc * 128].rearrange("p (kc j) -> p kc j", kc=kc)

        for l in layers:
            win = w_in_d[l].rearrange("(kc p) (c j) -> p c kc j", p=128, j=128)
            for gi_, nm in enumerate(["q", "k", "v", "u"]):
                prep(l, nm, [(ck(c * 1024, 8), win[:, 4 * gi_ + c]) for c in range(4)])
                if nm == "v":
                    prep(l, "wpg", [(lambda s: s[:, 0:512].rearrange("p (g d) -> p g d", g=4),
                                     w_pg_d[l].rearrange("g c d -> c g d"))])
            wao_r = w_ao_d[l].rearrange("(kc p) (n j) -> p n kc j", p=128, j=128)
            wpo_r = w_po_d[l].rearrange("(kc p) (n j) -> p n kc j", p=128, j=128)
            for n in range(8):
                prep(l, "e%d" % n, [(ck(0, 8), win[:, 16 + n]), (ck(1024, 8), win[:, 24 + n]),
                                    (ck(2048, 4), wao_r[:, n]), (ck(2560, 4), wpo_r[:, n])])
            wo = w_o_d[l].rearrange("(kc p) (n j) -> p n kc j", p=128, j=128)
            for i in range(2):
                prep(l, "wo%d" % i, [(ck(c * 1024, 8), wo[:, 4 * i + c]) for c in range(4)])
            wup = w_up_d[l].rearrange("(kc p) (c j) -> p c kc j", p=128, j=128)
            for i in range(11):
                prep(l, "up%d" % i, [(ck(0, 8), wup[:, 2 * i]), (ck(1024, 8), wup[:, 22 + 2 * i]),
                                     (ck(2048, 8), wup[:, 2 * i + 1]), (ck(3072, 8), wup[:, 22 + 2 * i + 1])])
            wdn = w_dn_d[l].rearrange("(kc p) (n j) -> p n kc j", p=128, j=128)
            for n in range(8):
                prep(l, "dn%d" % n, [(ck(0, 22), wdn[:, n])])
        P.barrier()

        wseq = [(l, gi) for _t in range(NT) for l in layers for gi in range(len(GROUPS))]
        wst = {"issued": 0, "next": 0}

        def w_issue():
            i = wst["issued"]
            l, gi = wseq[i]
            _, off, ncols = GROUPS[gi]
            s = i % NSLOT
            P.dma("sp", wslot[s][:, 0:ncols], wscr[l, :, off:off + ncols], "wld%d" % s,
                  writes=[("wslot", s)])
            wst["issued"] += 1

        def w_acquire(expect, keep=0):
            i = wst["next"]
            assert GROUPS[wseq[i][1]][0] == expect, (GROUPS[wseq[i][1]][0], expect)
            while wst["issued"] < min(len(wseq), i + NSLOT - keep):
                w_issue()
            wst["next"] += 1
            s = i % NSLOT
            return wslot[s], ("wslot", s)

        sq_ctr = [0]
        SS = 7

        def add_sq(src, first, last):
            b = sq[sq_ctr[0] % 2]
            k = ("sq", sq_ctr[0] % 2)
            sq_ctr[0] += 1
            P.act(V(b[:], k), src, AF.Square)
            P.mm(PS(SS), V(onesb[:], "onesb"), V(b[:], k), first, last)

        def finish_rstd():
            P.act(V(rr[:], "rr"), PS(SS), AF.Sqrt, bias=sm_eps)
            P.recip(PS(SS), V(rr[:], "rr"))

        epsb = sb("epsb", [128, 1], F32)
        P.memset("pool", V(epsb[:], "epsb"), EPS)
        sm_eps = V(epsb[:], "epsb")

        def xTv(c, lo=0, hi=T):
            return V(xT[:, c, lo:hi], ("xT", c))

        def hv(c, lo=0, hi=T):
            return V(h[:, c, lo:hi], ("h", c))

        def pre_norm(gname, l):
            for c in range(NCH):
                add_sq(xTv(c), c == 0, c == NCH - 1)
            finish_rstd()
            for c in range(NCH):
                P.stt(hv(c), xTv(c), sm(gname, l * 8 + c), PS(SS), ALU.mult, ALU.mult)

        tmp_ctr = [0]

        def post_norm_update():
            for n in range(NCH):
                tb_ = tmpn[tmp_ctr[0] % 2]
                k = ("tmpn", tmp_ctr[0] % 2)
                tmp_ctr[0] += 1
                P.tt("dve", V(tb_[:], k), r_f32(R3, "R3", 2 * n), PS(SS), ALU.mult)
                P.tt("pool", xTv(n), xTv(n), V(tb_[:], k), ALU.add)

        def layer_tile(l, tile):
            P.dma("sp", biasT[:].rearrange("p a b c -> p (a b c)"), bscr[l], "bld",
                  reads=[("bscr", l)], writes=["biasT"])
            pre_norm("g_mpre", l)
            slot, sk = w_acquire("q")
            for n in range(4):
                b = bank()
                for kc in range(8):
                    P.mm(PS(b), V(slot[:, (n * 8 + kc) * 128:(n * 8 + kc + 1) * 128], sk), hv(kc), kc == 0, kc == 7)
                P.cp("act", r_bf(R3, "R3", n), PS(b))
            slot, sk = w_acquire("k")
            for n in range(4):
                b = bank()
                for kc in range(8):
                    P.mm(PS(b), V(slot[:, (n * 8 + kc) * 128:(n * 8 + kc + 1) * 128], sk), hv(kc), kc == 0, kc == 7)
                P.cp("dve", V(kT[l][:, n, T:2 * T], ("kT", l)), PS(b))
            slot, sk = w_acquire("v")
            s4 = slot[:, 0:4096].rearrange("p (c kc j) -> p c kc j", c=4, kc=8)
            for tb in range(TB):
                b = bank()
                for kc in range(8):
                    P.mm(PS(b), hv(kc, tb * 128, (tb + 1) * 128), V(s4[:, :, kc, :], sk), kc == 0, kc == 7)
                eng = "act" if tb % 2 == 0 else "dve"
                P.cp(eng, V(Vb[l][:, 4 + tb, :, 0:64], ("Vb", l)),
                     V(ps[:, b, :].rearrange("p (h d) -> p h d", h=8), ("ps", b)))
            wpg, wpgk = w_acquire("wpg")
            slot, sk = w_acquire("u", keep=1)
            for g in range(4):
                b = bank()
                for kc in range(8):
                    P.mm(PS(b), V(slot[:, (g * 8 + kc) * 128:(g * 8 + kc + 1) * 128], sk), hv(kc), kc == 0, kc == 7)
                Ub = U[g % 2]
                uk = ("U", g % 2)
                P.cp("act", V(Ub[:, 16:16 + T], uk), PS(b))
                P.cp("pool", V(Ub[:, 0:16], uk), V(uhalo[l][:, g, :], ("uhalo", l)))
                P.cp("pool", V(uhalo[l][:, g, :], ("uhalo", l)), V(Ub[:, T:T + 16], uk))
                prev, pk, lo = Ub, uk, 0
                for step in range(g + 1):
                    sh = 1 << step
                    nb = sAB[step % 2]
                    nk = ("sAB", step % 2)
                    lo2 = lo + sh
                    P.tt("pool", V(nb[:, lo2:16 + T], nk), V(prev[:, lo2:16 + T], pk),
                         V(prev[:, lo2 - sh:16 + T - sh], pk), ALU.add)
                    prev, pk, lo = nb, nk, lo2
                w_ = 1 << (g + 1)
                pooled = r_bf(R3, "R3", 4 + g)
                P.stt(pooled, V(prev[:, 16:16 + T], pk), 1.0 / w_, V(Ub[:, 16:16 + T], uk), ALU.mult, ALU.subtract)
                if tile == 0:
                    ic = SM["invc"] + g * 16
                    P.tt("pool", V(t16[:], "t16"), V(prev[:, 16:32], pk), V(smalls[:, ic:ic + 16], "smalls"), ALU.mult)
                    P.tt("pool", V(pooled.ap[:, 0:16], *pooled.keys), V(t16[:], "t16"), V(Ub[:, 16:32], uk), ALU.subtract)
                b2 = bank()
                P.mm(PS(b2), V(wpg[:, g * 128:(g + 1) * 128], wpgk), pooled, True, True)
                P.act(r_bf(R3, "R3", 8 + g), PS(b2), AF.Identity, scale=sm("pscale", l * 4 + g))

            pt_ctr = [0]
            for qp in range(TB):
                pg = tile * TB + qp
                jlo = max(0, 4 - pg)
                OB = (5, 6)
                attn_tok = r_bf(R1, "R1", 7)
                for hg in range(2):
                    bd = bank(5)
                    ptm = {}
                    for hh in range(4):
                        hd = hg * 4 + hh
                        n, pb = hd // 2, (hd % 2) * 64
                        qv = V(R3[pb:pb + 64, n * 512 + qp * 128: n * 512 + (qp + 1) * 128], ("R3", n))
                        if jlo < 4:
                            bm = bank(5)
                            for j in range(jlo, 4):
                                kv = V(kT[l][pb:pb + 64, n, (qp + j) * 128:(qp + j + 1) * 128], ("kT", l))
                                P.mm(PS(bm, j * 128, (j + 1) * 128), kv, qv, True, False)
                                P.mm(PS(bm, j * 128, (j + 1) * 128), V(identb[:], "identb"),
                                     V(biasT[:, hd, j, :], "biasT"), False, True)
                            s_ = pt_ctr[0] % 5
                            pt_ctr[0] += 1
                            ptv = r_bf(R1, "R1", s_)
                            P.act(V(ptv.ap[:, jlo * 128:512], *ptv.keys), PS(bm, jlo * 128, 512), AF.Exp, scale=0.125)
                            ptm[hh] = ptv
                        kv = V(kT[l][pb:pb + 64, n, (qp + 4) * 128:(qp + 5) * 128], ("kT", l))
                        P.mm(PS(bd, hh * 128, (hh + 1) * 128), kv, qv, True, False)
                        P.mm(PS(bd, hh * 128, (hh + 1) * 128), V(identb[:], "identb"),
                             V(biasT[:, hd, 4, :], "biasT"), False, True)
                    ptd = r_bf(R1, "R1", 5 + hg)
                    P.act(ptd, PS(bd), AF.Exp, scale=0.125)
                    for hh in range(4):
                        hd = hg * 4 + hh
                        ov = PS(OB[hg], hh * 65, hh * 65 + 65)
                        for j in range(jlo, 4):
                            P.mm(ov, V(ptm[hh].ap[:, j * 128:(j + 1) * 128], *ptm[hh].keys),
                                 V(Vb[l][:, qp + j, hd, :], ("Vb", l)), j == jlo, False)
                        P.mm(ov, V(ptd.ap[:, hh * 128:(hh + 1) * 128], *ptd.keys),
                             V(Vb[l][:, qp + 4, hd, :], ("Vb", l)), jlo == 4, True)
                for hg in range(2):
                    o3 = ps[:, OB[hg], 0:260].rearrange("p (h d) -> p h d", h=4)
                    P.recip(V(rden[:, hg * 4:hg * 4 + 4], "rden"), V(o3[:, :, 64], ("ps", OB[hg])))
                    P.tt("dve", V(attn_tok.ap[:, hg * 256:(hg + 1) * 256].rearrange("p (h d) -> p h d", h=4), *attn_tok.keys),
                         V(o3[:, :, 0:64], ("ps", OB[hg])),
                         V(rden[:, hg * 4:hg * 4 + 4].unsqueeze(2).to_broadcast([128, 4, 64]), "rden"), ALU.mult)
                bt_ = bank(5)
                psb = ps[:, bt_, :].bitcast(BF16)
                for n in range(4):
                    P.tr(V(psb[:, n * 128:(n + 1) * 128], ("ps", bt_)),
                         V(attn_tok.ap[:, n * 128:(n + 1) * 128], *attn_tok.keys), V(identb[:], "identb"))
                outv = R3[:, 12 * 512:16 * 512].rearrange("p (n t) -> p n t", n=4)[:, :, qp * 128:(qp + 1) * 128]
                P.cp("act", V(outv, *[("R3", 12 + n) for n in range(4)]),
                     V(psb[:, 0:512].rearrange("p (n t) -> p n t", n=4), ("ps", bt_)))
            P.cp("pool", V(kT[l][:, :, 0:T], ("kT", l)), V(kT[l][:, :, T:2 * T], ("kT", l)))
            P.cp("pool", V(Vb[l][:, 0:4, :, 0:64], ("Vb", l)), V(Vb[l][:, 4:8, :, 0:64], ("Vb", l)))

            for n in range(NCH):
                gsl, gsk = w_acquire("e%d" % n)
                bga, bgb, bya, byb = bank(), bank(), bank(), bank()
                for kc in range(8):
                    c0 = kc * 128
                    P.mm(PS(bga), V(gsl[:, c0:c0 + 128], gsk), hv(kc), kc == 0, kc == 7)
                for kc in range(8):
                    c0 = 1024 + kc * 128
                    P.mm(PS(bgb), V(gsl[:, c0:c0 + 128], gsk), hv(kc), kc == 0, kc == 7)
                for kc in range(4):
                    c0 = 2048 + kc * 128
                    P.mm(PS(bya), V(gsl[:, c0:c0 + 128], gsk), r_bf(R3, "R3", 12 + kc), kc == 0, kc == 3)
                for kc in range(4):
                    c0 = 2560 + kc * 128
                    P.mm(PS(byb), V(gsl[:, c0:c0 + 128], gsk), r_bf(R3, "R3", 8 + kc), kc == 0, kc == 3)
                r = n % 2
                siga, sigb = r_f32(R1, "R1", 4 * r), r_f32(R1, "R1", 4 * r + 2)
                t1, t2 = r_f32(R1, "R1", 8 + 4 * r), r_f32(R1, "R1", 10 + 4 * r)
                P.act(siga, PS(bga), AF.Sigmoid, bias=sm("b_gate", l * 16 + n))
                P.act(sigb, PS(bgb), AF.Sigmoid, bias=sm("b_gate", l * 16 + 8 + n))
                P.tt("dve", t1, PS(bya), siga, ALU.mult)
                P.tt("dve", t2, PS(byb), sigb, ALU.mult)
                P.tt("pool", r_bf(R2, "R2", n), t1, t2, ALU.add)
            for n in range(NCH):
                if n % 4 == 0:
                    slot, sk = w_acquire("wo%d" % (n // 4))
                b = bank()
                for kc in range(8):
                    c0 = ((n % 4) * 8 + kc) * 128
                    P.mm(PS(b), V(slot[:, c0:c0 + 128], sk), r_bf(R2, "R2", kc), kc == 0, kc == 7)
                P.act(r_f32(R3, "R3", 2 * n), PS(b), AF.Identity, scale=sm("g_mpost", l * 8 + n))
                add_sq(PS(b), n == 0, n == NCH - 1)
            finish_rstd()
            post_norm_update()
            pre_norm("g_fpre", l)
            cw = lambda i, m: sm("convw", l * 132 + i * 44 + m)
            cb = lambda m: sm("convb", l * 44 + m)
            cv_ctr = [0]

            def conv(b, m, dst):
                P.act(dst, PS(b), AF.Identity, bias=cb(m), scale=cw(2, m))
                d1 = V(dst.ap[:, 1:T], *dst.keys)
                P.stt(d1, PS(b, 0, T - 1), cw(1, m), d1, ALU.mult, ALU.add)
                d2 = V(dst.ap[:, 2:T], *dst.keys)
                P.stt(d2, PS(b, 0, T - 2), cw(0, m), d2, ALU.mult, ALU.add)
                hk = ("chalo", l, m)
                d02 = V(dst.ap[:, 0:2], *dst.keys)
                P.stt(d02, V(chalo[l][:, m, :], hk), cw(0, m), d02, ALU.mult, ALU.add)
                d01 = V(dst.ap[:, 0:1], *dst.keys)
                P.stt(d01, V(chalo[l][:, m, 1:2], hk), cw(1, m), d01, ALU.mult, ALU.add)
                P.cp("act", V(chalo[l][:, m, :], hk), PS(b, T - 2, T))

            for j in range(11):
                slot, sk = w_acquire("up%d" % j)
                for a in range(2):
                    i = 2 * j + a
                    bv, bg = bank(), bank()
                    for kc in range(8):
                        c0 = ((2 * a) * 8 + kc) * 128
                        P.mm(PS(bv), V(slot[:, c0:c0 + 128], sk), hv(kc), kc == 0, kc == 7)
                    for kc in range(8):
                        c0 = ((2 * a + 1) * 8 + kc) * 128
                        P.mm(PS(bg), V(slot[:, c0:c0 + 128], sk), hv(kc), kc == 0, kc == 7)
                    r = cv_ctr[0] % 2
                    cv_ctr[0] += 1
                    aval, agate, gg = r_f32(R1, "R1", 2 * r), r_f32(R1, "R1", 4 + 2 * r), r_f32(R1, "R1", 8 + 2 * r)
                    conv(bv, i, aval)
                    conv(bg, 22 + i, agate)
                    P.act(gg, agate, AF.Gelu_apprx_tanh)
                    P.tt("pool", r_bf(R2, "R2", i), gg, aval, ALU.mult)
            for n in range(NCH):
                slot, sk = w_acquire("dn%d" % n)
                b = bank()
                for kc in range(NFF):
                    P.mm(PS(b), V(slot[:, kc * 128:(kc + 1) * 128], sk), r_bf(R2, "R2", kc), kc == 0, kc == NFF - 1)
                P.act(r_f32(R3, "R3", 2 * n), PS(b), AF.Identity, scale=sm("g_fpost", l * 8 + n))
                add_sq(PS(b), n == 0, n == NCH - 1)
            finish_rstd()
            post_norm_update()

        for tile in range(NT):
            t0 = tile * T
            for tb in range(TB):
                xi = xio[tb % 2]
                xk = ("xio", tb % 2)
                P.dma("sp", xi[:], x_d[t0 + tb * 128:t0 + (tb + 1) * 128, :], "xld%d" % (tb % 2), writes=[xk])
                for half in range(2):
                    b = bank()
                    for c4 in range(4):
                        c = half * 4 + c4
                        P.tr(PS(b, c4 * 128, (c4 + 1) * 128), V(xi[:, c * 128:(c + 1) * 128], xk), V(identf[:], "identf"))
                    P.cp("dve" if half == 0 else "act",
                         V(xT[:, half * 4:half * 4 + 4, tb * 128:(tb + 1) * 128], *[("xT", half * 4 + c4) for c4 in range(4)]),
                         V(ps[:, b, :].rearrange("p (c t) -> p c t", c=4), ("ps", b)))
            for l in layers:
                layer_tile(l, tile)
            for tb in range(TB):
                xo_ = xo[tb % 2]
                xk = ("xo", tb % 2)
                for half in range(2):
                    b = bank()
                    for c4 in range(4):
                        c = half * 4 + c4
                        P.tr(PS(b, c4 * 128, (c4 + 1) * 128), xTv(c, tb * 128, (tb + 1) * 128), V(identf[:], "identf"))
                    P.cp("dve" if half == 0 else "act", V(xo_[:, half * 512:(half + 1) * 512], xk), PS(b))
                P.dma("sp", out_d[t0 + tb * 128:t0 + (tb + 1) * 128, :], xo_[:], "xst%d" % (tb % 2),
                      reads=[xk], writes=["out"])
        P.barrier()
        P.build(st)
    return nc, P


def _host_layout(inputs, layers=(0, 1)):
    f = lambda k: np.asarray(inputs[k], dtype=np.float32)
    sm = np.zeros((128, NSM), np.float32)

    def put(name, arr2d):
        sm[:, SM[name]:SM[name] + arr2d.shape[0]] = arr2d.T
    put("g_mpre", f("norm_mix_pre").reshape(2 * 8, 128))
    put("g_mpost", f("norm_mix_post").reshape(2 * 8, 128))
    put("g_fpre", f("norm_ffn_pre").reshape(2 * 8, 128))
    put("g_fpost", f("norm_ffn_post").reshape(2 * 8, 128))
    put("b_gate", f("b_gate").reshape(2 * 16, 128))
    put("pscale", f("pool_scale").reshape(2 * 4, 128))
    put("convw", f("conv_w").reshape(2 * 3 * 44, 128))
    put("convb", f("conv_b").reshape(2 * 44, 128))
    invc = np.zeros((64, 128), np.float32)
    for g in range(4):
        for t in range(16):
            invc[g * 16 + t, :] = 1.0 / min(t + 1, 1 << (g + 1))
    put("invc", invc)
    m = np.arange(128)[:, None]
    i = np.arange(128)[None, :]
    rb = f("rel_bias")
    btab = np.zeros((2, 128, 8, 5, 128), np.float32)
    maskc = np.zeros((128, 8, 5, 128), np.float32)
    for j in range(5):
        idx = np.clip(128 * (4 - j) + i - m, -256, 256) + 256
        btab[:, :, :, j, :] = np.transpose(rb[:, :, idx], (0, 2, 1, 3))
    NEG = -240000.0
    maskc[0:64, :, 0, 64:128] = NEG
    maskc[64:128, :, 4, 0:64] = NEG
    return sm, btab.reshape(2, 128, 5120), maskc.reshape(128, 5120)


_CACHE = {}


def kernel(**inputs):
    NT = 16
    key = ("prog", NT)
    if key not in _CACHE:
        _CACHE[key] = build_program(NT)[0]
    nc = _CACHE[key]
    sm, btab, maskc = _host_layout(inputs)
    x = np.asarray(inputs["x"], dtype=np.float32)
    shared = {k: np.ascontiguousarray(np.asarray(inputs[k], dtype=np.float32)) for k in
              ["w_in", "w_attn_out", "w_pool_group", "w_pool_out", "w_o", "w_up", "w_down"]}
    shared.update({"btab": btab, "maskc": maskc, "smalls": sm})
    in_maps = []
    for b in range(8):
        m = dict(shared)
        m["x"] = np.ascontiguousarray(x[b])
        in_maps.append(m)
    res = run_bass_kernel_spmd(nc, in_maps, core_ids=list(range(8)))
    return np.stack([np.asarray(r["out"], dtype=np.float32) for r in res.results], axis=0)
```

```python
import numpy as np
from contextlib import ExitStack
import concourse.bass as bass
import concourse.mybir as mybir
from concourse.bass_utils import run_bass_kernel_spmd

F32 = mybir.dt.float32
BF16 = mybir.dt.bfloat16
AF = mybir.ActivationFunctionType
ALU = mybir.AluOpType

ENGS = ("pe", "act", "dve", "pool", "sp")

SEQ = 8192
D = 1024
T = 512
TB = 4
NCH = 8
DFF = 2816
NFF = 22
EPS = 1e-6
NSLOT = 4
import os
PIPE = os.environ.get('K_PIPE', '0') == '1'
DEFER = os.environ.get('K_DEFER', '1') == '1'
HNEW = os.environ.get('K_HNEW', '0') == '1'
SLOTC = 4096
GROUPS = []
_off = 0
for _nm, _nc in ([("wpg", 512), ("u", 4096), ("q", 4096), ("k", 4096), ("v", 4096)]
                 + [("e%d" % i, 3072) for i in range(8)] + [("wo0", 4096), ("wo1", 4096)]
                 + [("up%d" % i, 4096) for i in range(11)] + [("dn%d" % i, 2816) for i in range(8)]):
    GROUPS.append((_nm, _off, _nc))
    _off += _nc
WCOLS = _off

SM = {}
_o = 0
for _nm, _n in [("g_mpre", 16), ("g_mpost", 16), ("g_fpre", 16), ("g_fpost", 16), ("b_gate", 32),
                ("pscale", 8), ("convw", 2 * 3 * 44), ("convb", 2 * 44), ("invc", 64)]:
    SM[_nm] = _o
    _o += _n
NSM = _o


class V:
    __slots__ = ("ap", "keys")

    def __init__(self, ap, *keys):
        self.ap = ap
        self.keys = tuple(keys)


class Prog:
    def __init__(self, nc):
        self.nc = nc
        self.streams = {e: [] for e in ENGS}
        self.cnt = {e: 0 for e in ENGS}
        self.seen = {e: {} for e in ENGS}
        self.last_w = {}
        self.readers = {}
        self.sems = {}
        self.dma_cnt = {}
        self.n_wait = 0
        self.n_ins = 0
        self.stage = "setup"
        self.stages = {e: [] for e in ENGS}

    def _deps(self, eng, reads, writes, same_ok):
        deps = {}

        def need(tok):
            if tok[2] == eng and same_ok:
                return
            if tok[1] > deps.get(tok[0], 0):
                deps[tok[0]] = tok[1]
        for r in reads:
            lw = self.last_w.get(r)
            if lw is not None:
                need(lw)
            if isinstance(r, tuple) and r[0] == "ps":
                for rd in self.readers.get(r, ()):
                    if rd[2] != eng:
                        need(rd)
        for w in writes:
            lw = self.last_w.get(w)
            if lw is not None:
                need(lw)
            for rd in self.readers.get(w, ()):
                need(rd)
        out = []
        seen = self.seen[eng]
        for k, v in deps.items():
            if seen.get(k, 0) < v:
                seen[k] = v
                out.append((k, v))
        return out

    def _record(self, reads, writes, tok):
        for r in reads:
            lst = self.readers.setdefault(r, [])
            lst[:] = [x for x in lst if x[0] != tok[0]]
            lst.append(tok)
        for w in writes:
            self.last_w[w] = tok
            self.readers[w] = []

    def op(self, eng, fn, reads=(), writes=(), same_ok=False):
        waits = self._deps(eng, reads, writes, same_ok)
        self.cnt[eng] += 1
        tok = (("e", eng), self.cnt[eng], eng)
        self._record(reads, writes, tok)
        self.n_wait += len(waits)
        self.n_ins += 1
        self.stages[eng].append((self.stage, len(waits)))

        def emit(e, waits=waits, fn=fn, eng=eng):
            for k, v in waits:
                e.wait_ge(self.sems[k], v)
            fn(e).then_inc(self.sems[("e", eng)], 1)
        self.streams[eng].append(emit)

    def dma(self, queue, out, in_, semname, reads=(), writes=(), **kw):
        waits = self._deps(queue, reads, writes, False)
        k = ("d", semname)
        self.dma_cnt[k] = self.dma_cnt.get(k, 0) + 16
        tok = (k, self.dma_cnt[k], "dma")
        self._record(reads, writes, tok)
        self.n_wait += len(waits)
        self.n_ins += 1

        def emit(e, waits=waits, k=k):
            for kk, v in waits:
                e.wait_ge(self.sems[kk], v)
            e.dma_start(out=out, in_=in_, **kw).then_inc(self.sems[k], 16)
        self.streams[queue].append(emit)

    def barrier(self):
        targets = [(("e", e), self.cnt[e]) for e in ENGS if self.cnt[e] > 0]
        targets += [(k, v) for k, v in self.dma_cnt.items()]
        for eng in ENGS:
            waits = []
            seen = self.seen[eng]
            for k, v in targets:
                if k == ("e", eng):
                    continue
                if seen.get(k, 0) < v:
                    seen[k] = v
                    waits.append((k, v))

            def emit(e, waits=waits):
                for kk, v in waits:
                    e.wait_ge(self.sems[kk], v)
            self.streams[eng].append(emit)
        self.last_w.clear()
        self.readers.clear()

    @staticmethod
    def _k(*vs):
        ks = []
        for v in vs:
            if isinstance(v, V):
                ks.extend(v.keys)
        return ks

    @staticmethod
    def _a(v):
        return v.ap if isinstance(v, V) else v

    def mm(self, out, lhsT, rhs, start, stop):
        a = self._a
        self.op("pe", lambda e: e.matmul(a(out), a(lhsT), a(rhs), start=start, stop=stop),
                reads=self._k(lhsT, rhs), writes=self._k(out), same_ok=True)

    def tr(self, out, in_, ident):
        a = self._a
        self.op("pe", lambda e: e.transpose(a(out), a(in_), a(ident)),
                reads=self._k(in_, ident), writes=self._k(out), same_ok=True)

    def act(self, out, in_, func, bias=0.0, scale=1.0):
        a = self._a
        self.op("act", lambda e: e.activation(a(out), a(in_), func, bias=a(bias), scale=a(scale)),
                reads=self._k(in_, bias, scale), writes=self._k(out))

    def ts(self, eng, out, in0, s1, s2, op0, op1=None):
        a = self._a
        if op1 is None:
            f = lambda e: e.tensor_scalar(a(out), a(in0), a(s1), None, op0)
        else:
            f = lambda e: e.tensor_scalar(a(out), a(in0), a(s1), a(s2), op0, op1)
        self.op(eng, f, reads=self._k(in0, s1, s2), writes=self._k(out))

    def stt(self, out, in0, s, in1, op0, op1):
        a = self._a
        self.op("dve", lambda e: e.scalar_tensor_tensor(a(out), a(in0), a(s), a(in1), op0, op1),
                reads=self._k(in0, s, in1), writes=self._k(out))

    def tt(self, eng, out, in0, in1, op):
        a = self._a
        self.op(eng, lambda e: e.tensor_tensor(a(out), a(in0), a(in1), op),
                reads=self._k(in0, in1), writes=self._k(out))

    def cp(self, eng, out, in_):
        a = self._a
        if eng == "act":
            self.op(eng, lambda e: e.activation(a(out), a(in_), AF.Identity),
                    reads=self._k(in_), writes=self._k(out))
        else:
            self.op(eng, lambda e: e.tensor_copy(a(out), a(in_)), reads=self._k(in_), writes=self._k(out))

    def memset(self, eng, out, val):
        a = self._a
        self.op(eng, lambda e: e.memset(a(out), val), writes=self._k(out))

    def recip(self, out, in_):
        a = self._a
        self.op("dve", lambda e: e.reciprocal(a(out), a(in_)), reads=self._k(in_), writes=self._k(out))

    def build(self, stack):
        nc = self.nc
        for e in ENGS:
            self.sems[("e", e)] = stack.enter_context(nc.semaphore("s_" + e))
        for k in self.dma_cnt:
            self.sems[k] = stack.enter_context(nc.semaphore("d_" + str(k[1])))
        block = stack.enter_context(nc.Block())
        S = self.streams

        @block.tensor
        def _(e):
            for f in S["pe"]:
                f(e)

        @block.scalar
        def _(e):
            for f in S["act"]:
                f(e)

        @block.vector
        def _(e):
            for f in S["dve"]:
                f(e)

        @block.gpsimd
        def _(e):
            for f in S["pool"]:
                f(e)

        @block.sync
        def _(e):
            for f in S["sp"]:
                f(e)


def build_program(NT=16, layers=(0, 1)):
    nc = bass.Bass("TRN2", target_bir_lowering=False)
    dt_in = lambda name, shape, dt=F32: nc.dram_tensor(name, shape, dt, kind="ExternalInput").ap()
    x_d = dt_in("x", [SEQ, D])
    w_in_d = dt_in("w_in", [2, D, 4096])
    w_ao_d = dt_in("w_attn_out", [2, 512, D])
    w_pg_d = dt_in("w_pool_group", [2, 4, 128, 128])
    w_po_d = dt_in("w_pool_out", [2, 512, D])
    w_o_d = dt_in("w_o", [2, D, D])
    w_up_d = dt_in("w_up", [2, D, 2 * DFF])
    w_dn_d = dt_in("w_down", [2, DFF, D])
    btab_d = dt_in("btab", [2, 128, 5120])
    maskc_d = dt_in("maskc", [128, 5120])
    smalls_d = dt_in("smalls", [128, NSM])
    out_d = nc.dram_tensor("out", [SEQ, D], F32, kind="ExternalOutput").ap()
    wscr = nc.dram_tensor("wscr", [2, 128, WCOLS], BF16, kind="Internal").ap()
    bscr = nc.dram_tensor("bscr", [2, 128, 5120], BF16, kind="Internal").ap()

    P = Prog(nc)
    with ExitStack() as st:
        sb = lambda name, shape, dt: st.enter_context(nc.sbuf_tensor(name, shape, dt))
        wslot = [sb("wslot%d" % i, [128, SLOTC], BF16) for i in range(NSLOT)]
        xT = sb("xT", [128, NCH, T], F32)
        xio = [sb("xio%d" % i, [128, D], F32) for i in range(2)]
        xo = [sb("xo%d" % i, [128, D], F32) for i in range(2)]
        h = sb("h", [128, NCH, T], BF16)
        sq = [sb("sq%d" % i, [128, T], BF16) for i in range(2)]
        kT = {l: sb("kT%d" % l, [128, 4, 2 * T], BF16) for l in layers}
        Vb = {l: sb("Vb%d" % l, [128, 8, 8, 65], BF16) for l in layers}
        U = [sb("U%d" % i, [128, 16 + T], F32) for i in range(2)]
        uhalo = {l: sb("uhalo%d" % l, [128, 4, 16], F32) for l in layers}
        sAB = [sb("sAB%d" % i, [128, 16 + T], F32) for i in range(2)]
        R1 = sb("R1", [128, 16 * 512], BF16)
        R2 = sb("R2", [128, 22 * 512], BF16)
        R3 = sb("R3", [128, 16 * 512], BF16)
        tmpn = [sb("tmpn%d" % i, [128, T], F32) for i in range(2)]
        rr = sb("rr", [128, T], F32)
        cht = {(l, p_): sb("cht%d_%d" % (l, p_), [128, 44, 2, 2], F32) for l in layers for p_ in range(2)}
        fixA = sb("fixA", [128, 44, 2], F32)
        chalo = {l: sb("chalo%d" % l, [128, 44, 2], F32) for l in layers}
        fixB = sb("fixB", [128, 44, 2], F32)
        biasT = sb("biasT", [128, 8, 5, 128], BF16)
        smalls = sb("smalls_sb", [128, NSM], F32)
        identf = sb("identf", [128, 128], F32)
        identb = sb("identb", [128, 128], BF16)
        onesb = sb("onesb", [128, 128], BF16)
        rden = sb("rden", [128, 8], F32)
        t16 = sb("t16", [128, 16], F32)
        ps = st.enter_context(nc.psum_tensor("ps", [128, 8, 512], F32))

        def r_bf(R, name, s):
            return V(R[:, s * 512:(s + 1) * 512], (name, s))

        def r_f32(R, name, s):
            return V(R[:, s * 512:(s + 2) * 512].bitcast(F32), (name, s), (name, s + 1))

        def sm(name, col):
            c = SM[name] + col
            return V(smalls[:, c:c + 1], "smalls")

        bank_ctr = [0]

        def bank(n=7):
            b = bank_ctr[0] % n
            bank_ctr[0] += 1
            return b

        def PS(b, lo=0, hi=512):
            return V(ps[:, b, lo:hi], ("ps", b))

        P.dma("sp", smalls[:], smalls_d, "smld", writes=["smalls"])
        P.memset("pool", V(identf[:], "identf"), 1.0)
        P.op("pool", lambda e: e.affine_select(identf[:], identf[:], [[-1, 128]], ALU.is_equal, 0.0,
                                               base=0, channel_multiplier=1),
             reads=["identf"], writes=["identf"])
        P.cp("dve", V(identb[:], "identb"), V(identf[:], "identf"))
        P.memset("dve", V(onesb[:], "onesb"), 1.0 / 1024.0)
        for l in layers:
            P.memset("pool", V(kT[l][:], ("kT", l)), 0.0)
            P.memset("pool", V(Vb[l][:], ("Vb", l)), 0.0)
            P.memset("pool", V(Vb[l][:, :, :, 64:65], ("Vb", l)), 1.0)
            P.memset("pool", V(uhalo[l][:], ("uhalo", l)), 0.0)
            P.memset("pool", V(chalo[l][:], *[("chalo", l, m) for m in range(44)]), 0.0)
            for p_ in range(2):
                P.memset("pool", V(cht[(l, p_)][:], *[("cht", l, p_, m) for m in range(44)]), 0.0)
        stage_f = R2[:, 0:10240].bitcast(F32)
        stage_m = R1[:, 0:8192].bitcast(F32)
        r2keys = [("R2", s) for s in range(22)]
        r1keys = [("R1", s) for s in range(16)]
        r3keys = [("R3", s) for s in range(16)]
        stage_m2 = R3[:, 0:2048].bitcast(F32)
        P.dma("sp", stage_m, maskc_d[:, 0:4096], "mkld", writes=r1keys)
        P.dma("sp", stage_m2, maskc_d[:, 4096:5120], "mkld2", writes=r3keys)
        for l in layers:
            P.dma("sp", stage_f, btab_d[l], "btld", writes=r2keys)
            bt = biasT[:].rearrange("p a b c -> p (a b c)")
            P.stt(V(bt[:, 0:4096], "biasT"), V(stage_f[:, 0:4096], *r2keys), 8.0, V(stage_m, *r1keys),
                  ALU.mult, ALU.add)
            P.stt(V(bt[:, 4096:5120], "biasT"), V(stage_f[:, 4096:5120], *r2keys), 8.0, V(stage_m2, *r3keys),
                  ALU.mult, ALU.add)
            P.dma("sp", bscr[l], bt, "btst", reads=["biasT"], writes=[("bscr", l)])

        prep_i = [0]

        def prep(l, gname, parts):
            s = prep_i[0] % NSLOT
            prep_i[0] += 1
            gi = [g[0] for g in GROUPS].index(gname)
            _, off, ncols = GROUPS[gi]
            for dstf, src in parts:
                P.dma("pool", dstf(wslot[s]), src, "prep%d" % s, writes=[("wslot", s)])
            P.dma("sp", wscr[l, :, off:off + ncols], wslot[s][:, 0:ncols], "prepo%d" % s,
                  reads=[("wslot", s)], writes=[("wscr", l)])

        def v4(ap, c, kc):
            return ap.rearrange("p (c kc j) -> p c kc j", c=c, kc=kc)

        def ck(lo, kc):
            return lambda s: s[:, lo:lo + kc * 128].rearrange("p (kc j) -> p kc j", kc=kc)

        for l in layers:
            win = w_in_d[l].rearrange("(kc p) (c j) -> p c kc j", p=128, j=128)
            for gi_, nm in enumerate(["q", "k", "v", "u"]):
                prep(l, nm, [(ck(c * 1024, 8), win[:, 4 * gi_ + c]) for c in range(4)])
                if nm == "v":
                    prep(l, "wpg", [(lambda s: s[:, 0:512].rearrange("p (g d) -> p g d", g=4),
                                     w_pg_d[l].rearrange("g c d -> c g d"))])
            wao_r = w_ao_d[l].rearrange("(kc p) (n j) -> p n kc j", p=128, j=128)
            wpo_r = w_po_d[l].rearrange("(kc p) (n j) -> p n kc j", p=128, j=128)
            for n in range(8):
                prep(l, "e%d" % n, [(ck(0, 8), win[:, 16 + n]), (ck(1024, 8), win[:, 24 + n]),
                                    (ck(2048, 4), wao_r[:, n]), (ck(2560, 4), wpo_r[:, n])])
            wo = w_o_d[l].rearrange("(kc p) (n j) -> p n kc j", p=128, j=128)
            for i in range(2):
                prep(l, "wo%d" % i, [(ck(c * 1024, 8), wo[:, 4 * i + c]) for c in range(4)])
            wup = w_up_d[l].rearrange("(kc p) (c j) -> p c kc j", p=128, j=128)
            for i in range(11):
                prep(l, "up%d" % i, [(ck(0, 8), wup[:, 2 * i]), (ck(1024, 8), wup[:, 22 + 2 * i]),
                                     (ck(2048, 8), wup[:, 2 * i + 1]), (ck(3072, 8), wup[:, 22 + 2 * i + 1])])
            wdn = w_dn_d[l].rearrange("(kc p) (n j) -> p n kc j", p=128, j=128)
            for n in range(8):
                prep(l, "dn%d" % n, [(ck(0, 22), wdn[:, n])])
        P.barrier()

        wseq = [(l, gi) for _t in range(NT) for l in layers for gi in range(len(GROUPS))]
        wst = {"issued": 0, "next": 0}

        def w_issue():
            i = wst["issued"]
            l, gi = wseq[i]
            _, off, ncols = GROUPS[gi]
            s = i % NSLOT
            P.dma("sp", wslot[s][:, 0:ncols], wscr[l, :, off:off + ncols], "wld%d" % s,
                  writes=[("wslot", s)])
            wst["issued"] += 1

        def w_acquire(expect, keep=0):
            i = wst["next"]
            assert GROUPS[wseq[i][1]][0] == expect, (GROUPS[wseq[i][1]][0], expect)
            while wst["issued"] < min(len(wseq), i + NSLOT - keep):
                w_issue()
            wst["next"] += 1
            s = i % NSLOT
            return wslot[s], ("wslot", s)

        sq_ctr = [0]
        SS = 7

        def add_sq(src, first, last):
            b = sq[sq_ctr[0] % 2]
            k = ("sq", sq_ctr[0] % 2)
            sq_ctr[0] += 1
            P.act(V(b[:], k), src, AF.Square)
            P.mm(PS(SS), V(onesb[:], "onesb"), V(b[:], k), first, last)

        def finish_rstd():
            P.act(V(rr[:], "rr"), PS(SS), AF.Sqrt, bias=sm_eps)
            P.recip(PS(SS), V(rr[:], "rr"))

        epsb = sb("epsb", [128, 1], F32)
        P.memset("pool", V(epsb[:], "epsb"), EPS)
        sm_eps = V(epsb[:], "epsb")

        def xTv(c, lo=0, hi=T):
            return V(xT[:, c, lo:hi], ("xT", c))

        def hv(c, lo=0, hi=T):
            return V(h[:, c, lo:hi], ("h", c))

        def pre_norm(gname, l):
            for c in range(NCH):
                add_sq(xTv(c), c == 0, c == NCH - 1)
            finish_rstd()
            for c in range(NCH):
                P.stt(hv(c), xTv(c), sm(gname, l * 8 + c), PS(SS), ALU.mult, ALU.mult)

        tmp_ctr = [0]

        def post_norm_update():
            for n in range(NCH):
                tb_ = tmpn[tmp_ctr[0] % 2]
                k = ("tmpn", tmp_ctr[0] % 2)
                tmp_ctr[0] += 1
                P.tt("dve", V(tb_[:], k), r_f32(R3, "R3", 2 * n), PS(SS), ALU.mult)
                P.tt("pool", xTv(n), xTv(n), V(tb_[:], k), ALU.add)

        def layer_tile(l, tile):
            P.dma("sp", biasT[:].rearrange("p a b c -> p (a b c)"), bscr[l], "bld",
                  reads=[("bscr", l)], writes=["biasT"])
            P.stage = "A"
            pre_norm("g_mpre", l)
            P.stage = "B"
            wpg, wpgk = w_acquire("wpg")
            slot, sk = w_acquire("u", keep=1)
            for g in range(4):
                b = bank()
                for kc in range(8):
                    P.mm(PS(b), V(slot[:, (g * 8 + kc) * 128:(g * 8 + kc + 1) * 128], sk), hv(kc), kc == 0, kc == 7)
                Ub = U[g % 2]
                uk = ("U", g % 2)
                P.cp("act", V(Ub[:, 16:16 + T], uk), PS(b))
                P.cp("pool", V(Ub[:, 0:16], uk), V(uhalo[l][:, g, :], ("uhalo", l)))
                P.cp("pool", V(uhalo[l][:, g, :], ("uhalo", l)), V(Ub[:, T:T + 16], uk))
                prev, pk, lo = Ub, uk, 0
                for step in range(g + 1):
                    sh = 1 << step
                    nb = sAB[step % 2]
                    nk = ("sAB", step % 2)
                    lo2 = lo + sh
                    P.tt("pool", V(nb[:, lo2:16 + T], nk), V(prev[:, lo2:16 + T], pk),
                         V(prev[:, lo2 - sh:16 + T - sh], pk), ALU.add)
                    prev, pk, lo = nb, nk, lo2
                w_ = 1 << (g + 1)
                pooled = r_bf(R3, "R3", 4 + g)
                P.stt(pooled, V(prev[:, 16:16 + T], pk), 1.0 / w_, V(Ub[:, 16:16 + T], uk), ALU.mult, ALU.subtract)
                if tile == 0:
                    ic = SM["invc"] + g * 16
                    P.tt("pool", V(t16[:], "t16"), V(prev[:, 16:32], pk), V(smalls[:, ic:ic + 16], "smalls"), ALU.mult)
                    P.tt("pool", V(pooled.ap[:, 0:16], *pooled.keys), V(t16[:], "t16"), V(Ub[:, 16:32], uk), ALU.subtract)
                b2 = bank()
                P.mm(PS(b2), V(wpg[:, g * 128:(g + 1) * 128], wpgk), pooled, True, True)
                P.act(r_bf(R3, "R3", 8 + g), PS(b2), AF.Identity, scale=sm("pscale", l * 4 + g))
            P.memset("pool", V(R3[64:128, 0:2048], *[("R3", i) for i in range(4)]), 0.0)
            P.memset("pool", V(R3[0:64, 2048:4096], *[("R3", 4 + i) for i in range(4)]), 0.0)
            slot, sk = w_acquire("q")
            for n in range(4):
                b = bank()
                for kc in range(8):
                    P.mm(PS(b), V(slot[:, (n * 8 + kc) * 128:(n * 8 + kc + 1) * 128], sk), hv(kc), kc == 0, kc == 7)
                eng = "act" if n % 2 == 0 else "dve"
                P.cp(eng, V(R3[0:64, n * 512:(n + 1) * 512], ("R3", n)), V(ps[0:64, b, :], ("ps", b)))
                P.cp(eng, V(R3[64:128, (4 + n) * 512:(5 + n) * 512], ("R3", 4 + n)), V(ps[64:128, b, :], ("ps", b)))
            slot, sk = w_acquire("k")
            for n in range(4):
                b = bank()
                for kc in range(8):
                    P.mm(PS(b), V(slot[:, (n * 8 + kc) * 128:(n * 8 + kc + 1) * 128], sk), hv(kc), kc == 0, kc == 7)
                P.cp("dve" if n % 2 else "act", V(kT[l][:, n, T:2 * T], ("kT", l)), PS(b))
            slot, sk = w_acquire("v")
            s4 = slot[:, 0:4096].rearrange("p (c kc j) -> p c kc j", c=4, kc=8)
            for tb in range(TB):
                b = bank()
                for kc in range(8):
                    P.mm(PS(b), hv(kc, tb * 128, (tb + 1) * 128), V(s4[:, :, kc, :], sk), kc == 0, kc == 7)
                eng = "act" if tb % 2 == 0 else "dve"
                P.cp(eng, V(Vb[l][:, 4 + tb, :, 0:64], ("Vb", l)),
                     V(ps[:, b, :].rearrange("p (h d) -> p h d", h=8), ("ps", b)))

            P.stage = "C"
            OB = (5, 6)
            units = [(qp, hg) for qp in range(TB) for hg in range(2)]

            def scores(ui):
                qp, hg = units[ui]
                jlo = max(0, 4 - (tile * TB + qp))
                bd = bank(5)
                ptm = {}
                for hh in range(4):
                    hd = hg * 4 + hh
                    n = hd // 2
                    sl = n if hd % 2 == 0 else 4 + n
                    qv = V(R3[:, sl * 512 + qp * 128: sl * 512 + (qp + 1) * 128], ("R3", sl))
                    if jlo < 4:
                        bm = bank(5)
                        for j in range(jlo, 4):
                            kv = V(kT[l][:, n, (qp + j) * 128:(qp + j + 1) * 128], ("kT", l))
                            P.mm(PS(bm, j * 128, (j + 1) * 128), kv, qv, True, False)
                            P.mm(PS(bm, j * 128, (j + 1) * 128), V(identb[:], "identb"),
                                 V(biasT[:, hd, j, :], "biasT"), False, True)
                        ptv = r_bf(R1, "R1", (ui % 2) * 4 + hh)
                        P.act(V(ptv.ap[:, jlo * 128:512], *ptv.keys), PS(bm, jlo * 128, 512), AF.Exp, scale=0.125)
                        ptm[hh] = ptv
                    kv = V(kT[l][:, n, (qp + 4) * 128:(qp + 5) * 128], ("kT", l))
                    P.mm(PS(bd, hh * 128, (hh + 1) * 128), kv, qv, True, False)
                    P.mm(PS(bd, hh * 128, (hh + 1) * 128), V(identb[:], "identb"),
                         V(biasT[:, hd, 4, :], "biasT"), False, True)
                ptd = r_bf(R1, "R1", 8 + ui % 2)
                P.act(ptd, PS(bd), AF.Exp, scale=0.125)
                return ptm, ptd, jlo

            def pv(ui, pts):
                qp, hg = units[ui]
                ptm, ptd, jlo = pts
                for hh in range(4):
                    hd = hg * 4 + hh
                    ov = PS(OB[hg], hh * 65, hh * 65 + 65)
                    for j in range(jlo, 4):
                        P.mm(ov, V(ptm[hh].ap[:, j * 128:(j + 1) * 128], *ptm[hh].keys),
                             V(Vb[l][:, qp + j, hd, :], ("Vb", l)), j == jlo, False)
                    P.mm(ov, V(ptd.ap[:, hh * 128:(hh + 1) * 128], *ptd.keys),
                         V(Vb[l][:, qp + 4, hd, :], ("Vb", l)), jlo == 4, True)

            def normalize(qp, hg):
                attn_tok = r_bf(R1, "R1", 10 + qp % 2)
                o3 = ps[:, OB[hg], 0:260].rearrange("p (h d) -> p h d", h=4)
                P.recip(V(rden[:, hg * 4:hg * 4 + 4], "rden"), V(o3[:, :, 64], ("ps", OB[hg])))
                P.tt("dve", V(attn_tok.ap[:, hg * 256:(hg + 1) * 256].rearrange("p (h d) -> p h d", h=4), *attn_tok.keys),
                     V(o3[:, :, 0:64], ("ps", OB[hg])),
                     V(rden[:, hg * 4:hg * 4 + 4].unsqueeze(2).to_broadcast([128, 4, 64]), "rden"), ALU.mult)

            def transposes(qp):
                attn_tok = r_bf(R1, "R1", 10 + qp % 2)
                bt_ = bank(5)
                psb = ps[:, bt_, :].bitcast(BF16)
                for n in range(4):
                    P.tr(V(psb[:, n * 128:(n + 1) * 128], ("ps", bt_)),
                         V(attn_tok.ap[:, n * 128:(n + 1) * 128], *attn_tok.keys), V(identb[:], "identb"))
                outv = R3[:, 12 * 512:16 * 512].rearrange("p (n t) -> p n t", n=4)[:, :, qp * 128:(qp + 1) * 128]
                P.cp("act", V(outv, *[("R3", 12 + n) for n in range(4)]),
                     V(psb[:, 0:512].rearrange("p (n t) -> p n t", n=4), ("ps", bt_)))

            pend = scores(0)
            deferred = []
            for ui in range(len(units)):
                if not PIPE:
                    if ui > 0:
                        pend = scores(ui)
                    pv(ui, pend)
                    qp, hg = units[ui]
                    normalize(qp, hg)
                    if hg == 1:
                        transposes(qp)
                    continue
                nxt = scores(ui + 1) if ui + 1 < len(units) else None
                pv(ui, pend)
                for f in deferred:
                    f()
                deferred = []
                qp, hg = units[ui]
                normalize(qp, hg)
                if hg == 1:
                    if DEFER:
                        deferred.append(lambda qp=qp: transposes(qp))
                    else:
                        transposes(qp)
                pend = nxt
            for f in deferred:
                f()
            P.cp("pool", V(kT[l][:, :, 0:T], ("kT", l)), V(kT[l][:, :, T:2 * T], ("kT", l)))
            P.cp("pool", V(Vb[l][:, 0:4, :, 0:64], ("Vb", l)), V(Vb[l][:, 4:8, :, 0:64], ("Vb", l)))

            P.stage = "E"
            for n in range(NCH):
                gsl, gsk = w_acquire("e%d" % n)
                bga, bgb, bya, byb = bank(), bank(), bank(), bank()
                for kc in range(8):
                    c0 = kc * 128
                    P.mm(PS(bga), V(gsl[:, c0:c0 + 128], gsk), hv(kc), kc == 0, kc == 7)
                for kc in range(8):
                    c0 = 1024 + kc * 128
                    P.mm(PS(bgb), V(gsl[:, c0:c0 + 128], gsk), hv(kc), kc == 0, kc == 7)
                for kc in range(4):
                    c0 = 2048 + kc * 128
                    P.mm(PS(bya), V(gsl[:, c0:c0 + 128], gsk), r_bf(R3, "R3", 12 + kc), kc == 0, kc == 3)
                for kc in range(4):
                    c0 = 2560 + kc * 128
                    P.mm(PS(byb), V(gsl[:, c0:c0 + 128], gsk), r_bf(R3, "R3", 8 + kc), kc == 0, kc == 3)
                r = n % 2
                siga, sigb = r_f32(R1, "R1", 4 * r), r_f32(R1, "R1", 4 * r + 2)
                t1, t2 = r_f32(R1, "R1", 8 + 4 * r), r_f32(R1, "R1", 10 + 4 * r)
                P.act(siga, PS(bga), AF.Sigmoid, bias=sm("b_gate", l * 16 + n))
                P.act(sigb, PS(bgb), AF.Sigmoid, bias=sm("b_gate", l * 16 + 8 + n))
                P.tt("dve", t1, PS(bya), siga, ALU.mult)
                P.tt("dve", t2, PS(byb), sigb, ALU.mult)
                P.tt("pool", r_bf(R2, "R2", n), t1, t2, ALU.add)
            P.stage = "F"
            for n in range(NCH):
                if n % 4 == 0:
                    slot, sk = w_acquire("wo%d" % (n // 4))
                b = bank()
                for kc in range(8):
                    c0 = ((n % 4) * 8 + kc) * 128
                    P.mm(PS(b), V(slot[:, c0:c0 + 128], sk), r_bf(R2, "R2", kc), kc == 0, kc == 7)
                P.act(r_f32(R3, "R3", 2 * n), PS(b), AF.Identity, scale=sm("g_mpost", l * 8 + n))
                add_sq(PS(b), n == 0, n == NCH - 1)
            finish_rstd()
            post_norm_update()
            P.stage = "G"
            pre_norm("g_fpre", l)
            P.stage = "H"
            cw = lambda i, m: sm("convw", l * 132 + i * 44 + m)
            cb = lambda m: sm("convb", l * 44 + m)
            cv_ctr = [0]

            def conv_old(b, m, dst):
                P.act(dst, PS(b), AF.Identity, bias=cb(m), scale=cw(2, m))
                d1 = V(dst.ap[:, 1:T], *dst.keys)
                P.stt(d1, PS(b, 0, T - 1), cw(1, m), d1, ALU.mult, ALU.add)
                d2 = V(dst.ap[:, 2:T], *dst.keys)
                P.stt(d2, PS(b, 0, T - 2), cw(0, m), d2, ALU.mult, ALU.add)
                hk = ("chalo", l, m)
                d02 = V(dst.ap[:, 0:2], *dst.keys)
                P.stt(d02, V(chalo[l][:, m, :], hk), cw(0, m), d02, ALU.mult, ALU.add)
                d01 = V(dst.ap[:, 0:1], *dst.keys)
                P.stt(d01, V(chalo[l][:, m, 1:2], hk), cw(1, m), d01, ALU.mult, ALU.add)
                P.cp("act", V(chalo[l][:, m, :], hk), PS(b, T - 2, T))

            for j in range(11 if not HNEW else 0):
                slot, sk = w_acquire("up%d" % j)
                for a in range(2):
                    i = 2 * j + a
                    bv, bg = bank(), bank()
                    for kc in range(8):
                        c0 = ((2 * a) * 8 + kc) * 128
                        P.mm(PS(bv), V(slot[:, c0:c0 + 128], sk), hv(kc), kc == 0, kc == 7)
                    for kc in range(8):
                        c0 = ((2 * a + 1) * 8 + kc) * 128
                        P.mm(PS(bg), V(slot[:, c0:c0 + 128], sk), hv(kc), kc == 0, kc == 7)
                    r = cv_ctr[0] % 2
                    cv_ctr[0] += 1
                    aval, agate, gg = r_f32(R1, "R1", 2 * r), r_f32(R1, "R1", 4 + 2 * r), r_f32(R1, "R1", 8 + 2 * r)
                    conv_old(bv, i, aval)
                    conv_old(bg, 22 + i, agate)
                    P.act(gg, agate, AF.Gelu_apprx_tanh)
                    P.tt("pool", r_bf(R2, "R2", i), gg, aval, ALU.mult)
            if HNEW:
                par = tile % 2
                cur, prv = cht[(l, par)], cht[(l, 1 - par)]
                ckey = lambda p_, m: ("cht", l, p_, m)

                def conv(b, m, dst):
                    P.act(V(cur[:, m, 0, :], ckey(par, m)), PS(b, 0, 2), AF.Identity)
                    P.act(V(cur[:, m, 1, :], ckey(par, m)), PS(b, T - 2, T), AF.Identity)
                    P.act(dst, PS(b), AF.Identity, bias=cb(m), scale=cw(2, m))
                    d1 = V(dst.ap[:, 1:T], *dst.keys)
                    P.stt(d1, PS(b, 0, T - 1), cw(1, m), d1, ALU.mult, ALU.add)
                    d2 = V(dst.ap[:, 2:T], *dst.keys)
                    P.stt(d2, PS(b, 0, T - 2), cw(0, m), d2, ALU.mult, ALU.add)

                pending = None
                for j in range(11):
                    slot, sk = w_acquire("up%d" % j)
                    for a in range(2):
                        i = 2 * j + a
                        bv, bg = bank(), bank()
                        for kc in range(8):
                            c0 = ((2 * a) * 8 + kc) * 128
                            P.mm(PS(bv), V(slot[:, c0:c0 + 128], sk), hv(kc), kc == 0, kc == 7)
                        for kc in range(8):
                            c0 = ((2 * a + 1) * 8 + kc) * 128
                            P.mm(PS(bg), V(slot[:, c0:c0 + 128], sk), hv(kc), kc == 0, kc == 7)
                        r = i % 4
                        aval, agate = r_f32(R1, "R1", 4 * r), r_f32(R1, "R1", 4 * r + 2)
                        conv(bv, i, aval)
                        conv(bg, 22 + i, agate)
                        if pending is not None:
                            pending()

                        def fin(i=i, aval=aval, agate=agate):
                            P.act(agate, agate, AF.Gelu_apprx_tanh)
                            P.tt("pool", r_bf(R2, "R2", i), agate, aval, ALU.mult)
                        pending = fin
                pending()
                allk = lambda p_: [ckey(p_, m) for m in range(44)]
                c0w = SM["convw"] + l * 132
                wb = lambda i: V(smalls[:, c0w + i * 44:c0w + (i + 1) * 44].unsqueeze(2).to_broadcast([128, 44, 2]), "smalls")
                w1v = V(smalls[:, c0w + 44:c0w + 88], "smalls")
                bb = V(smalls[:, SM["convb"] + l * 44:SM["convb"] + (l + 1) * 44].unsqueeze(2).to_broadcast([128, 44, 2]), "smalls")
                HD = V(cur[:, :, 0, :], *allk(par))
                TL = V(prv[:, :, 1, :], *allk(1 - par))
                fA, fB = V(fixA[:], "fixA"), V(fixB[:], "fixB")
                P.tt("dve", fA, HD, wb(2), ALU.mult)
                P.tt("dve", fA, fA, bb, ALU.add)
                P.tt("dve", fB, TL, wb(0), ALU.mult)
                P.tt("dve", fA, fA, fB, ALU.add)
                P.tt("dve", V(fixB[:, :, 0], "fixB"), V(prv[:, :, 1, 1], *allk(1 - par)), w1v, ALU.mult)
                P.tt("dve", V(fixB[:, :, 1], "fixB"), V(cur[:, :, 0, 0], *allk(par)), w1v, ALU.mult)
                P.tt("dve", fA, fA, fB, ALU.add)
                P.act(V(fixB[:, 22:44, :], "fixB"), V(fixA[:, 22:44, :], "fixA"), AF.Gelu_apprx_tanh)
                r2v = R2[:, :].rearrange("p (i t) -> p i t", i=22)[:, :, 0:2]
                P.tt("dve", V(r2v, *[("R2", i) for i in range(22)]), V(fixB[:, 22:44, :], "fixB"), V(fixA[:, 0:22, :], "fixA"), ALU.mult)

            P.stage = "I"
            for n in range(NCH):
                slot, sk = w_acquire("dn%d" % n)
                b = bank()
                for kc in range(NFF):
                    P.mm(PS(b), V(slot[:, kc * 128:(kc + 1) * 128], sk), r_bf(R2, "R2", kc), kc == 0, kc == NFF - 1)
                P.act(r_f32(R3, "R3", 2 * n), PS(b), AF.Identity, scale=sm("g_fpost", l * 8 + n))
                add_sq(PS(b), n == 0, n == NCH - 1)
            finish_rstd()
            post_norm_update()

        for tile in range(NT):
            t0 = tile * T
            P.stage = "xin"
            for tb in range(TB):
                xi = xio[tb % 2]
                xk = ("xio", tb % 2)
                P.dma("sp", xi[:], x_d[t0 + tb * 128:t0 + (tb + 1) * 128, :], "xld%d" % (tb % 2), writes=[xk])
                for half in range(2):
                    b = bank()
                    for c4 in range(4):
                        c = half * 4 + c4
                        P.tr(PS(b, c4 * 128, (c4 + 1) * 128), V(xi[:, c * 128:(c + 1) * 128], xk), V(identf[:], "identf"))
                    P.cp("dve" if half == 0 else "act",
                         V(xT[:, half * 4:half * 4 + 4, tb * 128:(tb + 1) * 128], *[("xT", half * 4 + c4) for c4 in range(4)]),
                         V(ps[:, b, :].rearrange("p (c t) -> p c t", c=4), ("ps", b)))
            for l in layers:
                layer_tile(l, tile)
            P.stage = "xout"
            for tb in range(TB):
                xo_ = xo[tb % 2]
                xk = ("xo", tb % 2)
                for half in range(2):
                    b = bank()
                    for c4 in range(4):
                        c = half * 4 + c4
                        P.tr(PS(b, c4 * 128, (c4 + 1) * 128), xTv(c, tb * 128, (tb + 1) * 128), V(identf[:], "identf"))
                    P.cp("dve" if half == 0 else "act", V(xo_[:, half * 512:(half + 1) * 512], xk), PS(b))
                P.dma("sp", out_d[t0 + tb * 128:t0 + (tb + 1) * 128, :], xo_[:], "xst%d" % (tb % 2),
                      reads=[xk], writes=["out"])
        P.barrier()
        P.build(st)
    return nc, P


def _host_layout(inputs, layers=(0, 1)):
    f = lambda k: np.asarray(inputs[k], dtype=np.float32)
    sm = np.zeros((128, NSM), np.float32)

    def put(name, arr2d):
        sm[:, SM[name]:SM[name] + arr2d.shape[0]] = arr2d.T
    put("g_mpre", f("norm_mix_pre").reshape(2 * 8, 128))
    put("g_mpost", f("norm_mix_post").reshape(2 * 8, 128))
    put("g_fpre", f("norm_ffn_pre").reshape(2 * 8, 128))
    put("g_fpost", f("norm_ffn_post").reshape(2 * 8, 128))
    put("b_gate", f("b_gate").reshape(2 * 16, 128))
    put("pscale", f("pool_scale").reshape(2 * 4, 128))
    put("convw", f("conv_w").reshape(2 * 3 * 44, 128))
    put("convb", f("conv_b").reshape(2 * 44, 128))
    invc = np.zeros((64, 128), np.float32)
    for g in range(4):
        for t in range(16):
            invc[g * 16 + t, :] = 1.0 / min(t + 1, 1 << (g + 1))
    put("invc", invc)
    m = np.arange(128)[:, None]
    i = np.arange(128)[None, :]
    rb = f("rel_bias")
    btab = np.zeros((2, 128, 8, 5, 128), np.float32)
    maskc = np.zeros((128, 8, 5, 128), np.float32)
    for j in range(5):
        idx = np.clip(128 * (4 - j) + i - m, -256, 256) + 256
        btab[:, :, :, j, :] = np.transpose(rb[:, :, idx], (0, 2, 1, 3))
    NEG = -240000.0
    maskc[0:64, :, 0, 64:128] = NEG
    maskc[64:128, :, 4, 0:64] = NEG
    return sm, btab.reshape(2, 128, 5120), maskc.reshape(128, 5120)


_CACHE = {}


def kernel(**inputs):
    NT = 16
    key = ("prog", NT)
    if key not in _CACHE:
        _CACHE[key] = build_program(NT)[0]
    nc = _CACHE[key]
    sm, btab, maskc = _host_layout(inputs)
    x = np.asarray(inputs["x"], dtype=np.float32)
    shared = {k: np.ascontiguousarray(np.asarray(inputs[k], dtype=np.float32)) for k in
              ["w_in", "w_attn_out", "w_pool_group", "w_pool_out", "w_o", "w_up", "w_down"]}
    shared.update({"btab": btab, "maskc": maskc, "smalls": sm})
    in_maps = []
    for b in range(8):
        m = dict(shared)
        m["x"] = np.ascontiguousarray(x[b])
        in_maps.append(m)
    res = run_bass_kernel_spmd(nc, in_maps, core_ids=list(range(8)))
    return np.stack([np.asarray(r["out"], dtype=np.float32) for r in res.results], axis=0)
```

```python
import numpy as np
from contextlib import ExitStack
import concourse.bass as bass
import concourse.mybir as mybir
from concourse.bass_utils import run_bass_kernel_spmd

F32 = mybir.dt.float32
BF16 = mybir.dt.bfloat16
AF = mybir.ActivationFunctionType
ALU = mybir.AluOpType

ENGS = ("pe", "act", "dve", "pool", "sp")

SEQ = 8192
D = 1024
T = 512
TB = 4
NCH = 8
DFF = 2816
NFF = 22
EPS = 1e-6
NSLOT = 4
import os
PIPE = os.environ.get('K_PIPE', '1') == '1'
DEFER = os.environ.get('K_DEFER', '1') == '1'
HNEW = os.environ.get('K_HNEW', '0') == '1'
SLOTC = 4096
GROUPS = []
_off = 0
for _nm, _nc in ([("wpg", 512), ("u", 4096), ("q", 4096), ("k", 4096), ("v", 4096)]
                 + [("e%d" % i, 3072) for i in range(8)] + [("wo0", 4096), ("wo1", 4096)]
                 + [("up%d" % i, 4096) for i in range(11)] + [("dn%d" % i, 2816) for i in range(8)]):
    GROUPS.append((_nm, _off, _nc))
    _off += _nc
WCOLS = _off

SM = {}
_o = 0
for _nm, _n in [("g_mpre", 16), ("g_mpost", 16), ("g_fpre", 16), ("g_fpost", 16), ("b_gate", 32),
                ("pscale", 8), ("convw", 2 * 3 * 44), ("convb", 2 * 44), ("invc", 64)]:
    SM[_nm] = _o
    _o += _n
NSM = _o


class V:
    __slots__ = ("ap", "keys")

    def __init__(self, ap, *keys):
        self.ap = ap
        self.keys = tuple(keys)


class Prog:
    def __init__(self, nc):
        self.nc = nc
        self.streams = {e: [] for e in ENGS}
        self.cnt = {e: 0 for e in ENGS}
        self.seen = {e: {} for e in ENGS}
        self.last_w = {}
        self.readers = {}
        self.sems = {}
        self.dma_cnt = {}
        self.n_wait = 0
        self.n_ins = 0
        self.stage = "setup"
        self.stages = {e: [] for e in ENGS}

    def _deps(self, eng, reads, writes, same_ok):
        deps = {}

        def need(tok):
            if tok[2] == eng and same_ok:
                return
            if tok[1] > deps.get(tok[0], 0):
                deps[tok[0]] = tok[1]
        for r in reads:
            lw = self.last_w.get(r)
            if lw is not None:
                need(lw)
            if isinstance(r, tuple) and r[0] == "ps":
                for rd in self.readers.get(r, ()):
                    if rd[2] != eng:
                        need(rd)
        for w in writes:
            lw = self.last_w.get(w)
            if lw is not None:
                need(lw)
            for rd in self.readers.get(w, ()):
                need(rd)
        out = []
        seen = self.seen[eng]
        for k, v in deps.items():
            if seen.get(k, 0) < v:
                seen[k] = v
                out.append((k, v))
        return out

    def _record(self, reads, writes, tok):
        for r in reads:
            lst = self.readers.setdefault(r, [])
            lst[:] = [x for x in lst if x[0] != tok[0]]
            lst.append(tok)
        for w in writes:
            self.last_w[w] = tok
            self.readers[w] = []

    def op(self, eng, fn, reads=(), writes=(), same_ok=False):
        waits = self._deps(eng, reads, writes, same_ok)
        self.cnt[eng] += 1
        tok = (("e", eng), self.cnt[eng], eng)
        self._record(reads, writes, tok)
        self.n_wait += len(waits)
        self.n_ins += 1
        self.stages[eng].append((self.stage, len(waits)))

        def emit(e, waits=waits, fn=fn, eng=eng):
            for k, v in waits:
                e.wait_ge(self.sems[k], v)
            fn(e).then_inc(self.sems[("e", eng)], 1)
        self.streams[eng].append(emit)

    def dma(self, queue, out, in_, semname, reads=(), writes=(), **kw):
        waits = self._deps(queue, reads, writes, False)
        k = ("d", semname)
        self.dma_cnt[k] = self.dma_cnt.get(k, 0) + 16
        tok = (k, self.dma_cnt[k], "dma")
        self._record(reads, writes, tok)
        self.n_wait += len(waits)
        self.n_ins += 1

        def emit(e, waits=waits, k=k):
            for kk, v in waits:
                e.wait_ge(self.sems[kk], v)
            e.dma_start(out=out, in_=in_, **kw).then_inc(self.sems[k], 16)
        self.streams[queue].append(emit)

    def barrier(self):
        targets = [(("e", e), self.cnt[e]) for e in ENGS if self.cnt[e] > 0]
        targets += [(k, v) for k, v in self.dma_cnt.items()]
        for eng in ENGS:
            waits = []
            seen = self.seen[eng]
            for k, v in targets:
                if k == ("e", eng):
                    continue
                if seen.get(k, 0) < v:
                    seen[k] = v
                    waits.append((k, v))

            def emit(e, waits=waits):
                for kk, v in waits:
                    e.wait_ge(self.sems[kk], v)
            self.streams[eng].append(emit)
        self.last_w.clear()
        self.readers.clear()

    @staticmethod
    def _k(*vs):
        ks = []
        for v in vs:
            if isinstance(v, V):
                ks.extend(v.keys)
        return ks

    @staticmethod
    def _a(v):
        return v.ap if isinstance(v, V) else v

    def mm(self, out, lhsT, rhs, start, stop):
        a = self._a
        self.op("pe", lambda e: e.matmul(a(out), a(lhsT), a(rhs), start=start, stop=stop),
                reads=self._k(lhsT, rhs), writes=self._k(out), same_ok=True)

    def tr(self, out, in_, ident):
        a = self._a
        self.op("pe", lambda e: e.transpose(a(out), a(in_), a(ident)),
                reads=self._k(in_, ident), writes=self._k(out), same_ok=True)

    def act(self, out, in_, func, bias=0.0, scale=1.0):
        a = self._a
        self.op("act", lambda e: e.activation(a(out), a(in_), func, bias=a(bias), scale=a(scale)),
                reads=self._k(in_, bias, scale), writes=self._k(out))

    def ts(self, eng, out, in0, s1, s2, op0, op1=None):
        a = self._a
        if op1 is None:
            f = lambda e: e.tensor_scalar(a(out), a(in0), a(s1), None, op0)
        else:
            f = lambda e: e.tensor_scalar(a(out), a(in0), a(s1), a(s2), op0, op1)
        self.op(eng, f, reads=self._k(in0, s1, s2), writes=self._k(out))

    def stt(self, out, in0, s, in1, op0, op1):
        a = self._a
        self.op("dve", lambda e: e.scalar_tensor_tensor(a(out), a(in0), a(s), a(in1), op0, op1),
                reads=self._k(in0, s, in1), writes=self._k(out))

    def tt(self, eng, out, in0, in1, op):
        a = self._a
        self.op(eng, lambda e: e.tensor_tensor(a(out), a(in0), a(in1), op),
                reads=self._k(in0, in1), writes=self._k(out))

    def cp(self, eng, out, in_):
        a = self._a
        if eng == "act":
            self.op(eng, lambda e: e.activation(a(out), a(in_), AF.Identity),
                    reads=self._k(in_), writes=self._k(out))
        else:
            self.op(eng, lambda e: e.tensor_copy(a(out), a(in_)), reads=self._k(in_), writes=self._k(out))

    def memset(self, eng, out, val):
        a = self._a
        self.op(eng, lambda e: e.memset(a(out), val), writes=self._k(out))

    def recip(self, out, in_):
        a = self._a
        self.op("dve", lambda e: e.reciprocal(a(out), a(in_)), reads=self._k(in_), writes=self._k(out))

    def build(self, stack):
        nc = self.nc
        for e in ENGS:
            self.sems[("e", e)] = stack.enter_context(nc.semaphore("s_" + e))
        for k in self.dma_cnt:
            self.sems[k] = stack.enter_context(nc.semaphore("d_" + str(k[1])))
        block = stack.enter_context(nc.Block())
        S = self.streams

        @block.tensor
        def _(e):
            for f in S["pe"]:
                f(e)

        @block.scalar
        def _(e):
            for f in S["act"]:
                f(e)

        @block.vector
        def _(e):
            for f in S["dve"]:
                f(e)

        @block.gpsimd
        def _(e):
            for f in S["pool"]:
                f(e)

        @block.sync
        def _(e):
            for f in S["sp"]:
                f(e)


def build_program(NT=16, layers=(0, 1)):
    nc = bass.Bass("TRN2", target_bir_lowering=False)
    dt_in = lambda name, shape, dt=F32: nc.dram_tensor(name, shape, dt, kind="ExternalInput").ap()
    x_d = dt_in("x", [SEQ, D])
    w_in_d = dt_in("w_in", [2, D, 4096])
    w_ao_d = dt_in("w_attn_out", [2, 512, D])
    w_pg_d = dt_in("w_pool_group", [2, 4, 128, 128])
    w_po_d = dt_in("w_pool_out", [2, 512, D])
    w_o_d = dt_in("w_o", [2, D, D])
    w_up_d = dt_in("w_up", [2, D, 2 * DFF])
    w_dn_d = dt_in("w_down", [2, DFF, D])
    btab_d = dt_in("btab", [2, 128, 5120])
    maskc_d = dt_in("maskc", [128, 5120])
    smalls_d = dt_in("smalls", [128, NSM])
    out_d = nc.dram_tensor("out", [SEQ, D], F32, kind="ExternalOutput").ap()
    wscr = nc.dram_tensor("wscr", [2, 128, WCOLS], BF16, kind="Internal").ap()
    bscr = nc.dram_tensor("bscr", [2, 128, 5120], BF16, kind="Internal").ap()

    P = Prog(nc)
    with ExitStack() as st:
        sb = lambda name, shape, dt: st.enter_context(nc.sbuf_tensor(name, shape, dt))
        wslot = [sb("wslot%d" % i, [128, SLOTC], BF16) for i in range(NSLOT)]
        xT = sb("xT", [128, NCH, T], F32)
        xio = [sb("xio%d" % i, [128, D], F32) for i in range(2)]
        xo = [sb("xo%d" % i, [128, D], F32) for i in range(2)]
        h = sb("h", [128, NCH, T], BF16)
        sq = [sb("sq%d" % i, [128, T], BF16) for i in range(2)]
        kT = {l: sb("kT%d" % l, [128, 4, 2 * T], BF16) for l in layers}
        Vb = {l: sb("Vb%d" % l, [128, 8, 8, 65], BF16) for l in layers}
        U = [sb("U%d" % i, [128, 16 + T], F32) for i in range(2)]
        uhalo = {l: sb("uhalo%d" % l, [128, 4, 16], F32) for l in layers}
        sAB = [sb("sAB%d" % i, [128, 16 + T], F32) for i in range(2)]
        R1 = sb("R1", [128, 16 * 512], BF16)
        R2 = sb("R2", [128, 22 * 512], BF16)
        R3 = sb("R3", [128, 16 * 512], BF16)
        tmpn = [sb("tmpn%d" % i, [128, T], F32) for i in range(2)]
        rr = sb("rr", [128, T], F32)
        cht = {(l, p_): sb("cht%d_%d" % (l, p_), [128, 44, 2, 2], F32) for l in layers for p_ in range(2)}
        fixA = sb("fixA", [128, 44, 2], F32)
        chalo = {l: sb("chalo%d" % l, [128, 44, 2], F32) for l in layers}
        fixB = sb("fixB", [128, 44, 2], F32)
        biasT = sb("biasT", [128, 8, 5, 128], BF16)
        smalls = sb("smalls_sb", [128, NSM], F32)
        identf = sb("identf", [128, 128], F32)
        identb = sb("identb", [128, 128], BF16)
        onesb = sb("onesb", [128, 128], BF16)
        rden = sb("rden", [128, 8], F32)
        t16 = sb("t16", [128, 16], F32)
        ps = st.enter_context(nc.psum_tensor("ps", [128, 8, 512], F32))

        def r_bf(R, name, s):
            return V(R[:, s * 512:(s + 1) * 512], (name, s))

        def r_f32(R, name, s):
            return V(R[:, s * 512:(s + 2) * 512].bitcast(F32), (name, s), (name, s + 1))

        def sm(name, col):
            c = SM[name] + col
            return V(smalls[:, c:c + 1], "smalls")

        bank_ctr = [0]

        def bank(n=7):
            b = bank_ctr[0] % n
            bank_ctr[0] += 1
            return b

        def PS(b, lo=0, hi=512):
            return V(ps[:, b, lo:hi], ("ps", b))

        P.dma("sp", smalls[:], smalls_d, "smld", writes=["smalls"])
        P.memset("pool", V(identf[:], "identf"), 1.0)
        P.op("pool", lambda e: e.affine_select(identf[:], identf[:], [[-1, 128]], ALU.is_equal, 0.0,
                                               base=0, channel_multiplier=1),
             reads=["identf"], writes=["identf"])
        P.cp("dve", V(identb[:], "identb"), V(identf[:], "identf"))
        P.memset("dve", V(onesb[:], "onesb"), 1.0 / 1024.0)
        for l in layers:
            P.memset("pool", V(kT[l][:], ("kT", l)), 0.0)
            P.memset("pool", V(Vb[l][:], ("Vb", l)), 0.0)
            P.memset("pool", V(Vb[l][:, :, :, 64:65], ("Vb", l)), 1.0)
            P.memset("pool", V(uhalo[l][:], ("uhalo", l)), 0.0)
            P.memset("pool", V(chalo[l][:], *[("chalo", l, m) for m in range(44)]), 0.0)
            for p_ in range(2):
                P.memset("pool", V(cht[(l, p_)][:], *[("cht", l, p_, m) for m in range(44)]), 0.0)
        stage_f = R2[:, 0:10240].bitcast(F32)
        stage_m = R1[:, 0:8192].bitcast(F32)
        r2keys = [("R2", s) for s in range(22)]
        r1keys = [("R1", s) for s in range(16)]
        r3keys = [("R3", s) for s in range(16)]
        stage_m2 = R3[:, 0:2048].bitcast(F32)
        P.dma("sp", stage_m, maskc_d[:, 0:4096], "mkld", writes=r1keys)
        P.dma("sp", stage_m2, maskc_d[:, 4096:5120], "mkld2", writes=r3keys)
        for l in layers:
            P.dma("sp", stage_f, btab_d[l], "btld", writes=r2keys)
            bt = biasT[:].rearrange("p a b c -> p (a b c)")
            P.stt(V(bt[:, 0:4096], "biasT"), V(stage_f[:, 0:4096], *r2keys), 8.0, V(stage_m, *r1keys),
                  ALU.mult, ALU.add)
            P.stt(V(bt[:, 4096:5120], "biasT"), V(stage_f[:, 4096:5120], *r2keys), 8.0, V(stage_m2, *r3keys),
                  ALU.mult, ALU.add)
            P.dma("sp", bscr[l], bt, "btst", reads=["biasT"], writes=[("bscr", l)])

        prep_i = [0]

        def prep(l, gname, parts):
            s = prep_i[0] % NSLOT
            prep_i[0] += 1
            gi = [g[0] for g in GROUPS].index(gname)
            _, off, ncols = GROUPS[gi]
            for dstf, src in parts:
                P.dma("pool", dstf(wslot[s]), src, "prep%d" % s, writes=[("wslot", s)])
            P.dma("sp", wscr[l, :, off:off + ncols], wslot[s][:, 0:ncols], "prepo%d" % s,
                  reads=[("wslot", s)], writes=[("wscr", l)])

        def v4(ap, c, kc):
            return ap.rearrange("p (c kc j) -> p c kc j", c=c, kc=kc)

        def ck(lo, kc):
            return lambda s: s[:, lo:lo + kc * 128].rearrange("p (kc j) -> p kc j", kc=kc)

        for l in layers:
            win = w_in_d[l].rearrange("(kc p) (c j) -> p c kc j", p=128, j=128)
            for gi_, nm in enumerate(["q", "k", "v", "u"]):
                prep(l, nm, [(ck(c * 1024, 8), win[:, 4 * gi_ + c]) for c in range(4)])
                if nm == "v":
                    prep(l, "wpg", [(lambda s: s[:, 0:512].rearrange("p (g d) -> p g d", g=4),
                                     w_pg_d[l].rearrange("g c d -> c g d"))])
            wao_r = w_ao_d[l].rearrange("(kc p) (n j) -> p n kc j", p=128, j=128)
            wpo_r = w_po_d[l].rearrange("(kc p) (n j) -> p n kc j", p=128, j=128)
            for n in range(8):
                prep(l, "e%d" % n, [(ck(0, 8), win[:, 16 + n]), (ck(1024, 8), win[:, 24 + n]),
                                    (ck(2048, 4), wao_r[:, n]), (ck(2560, 4), wpo_r[:, n])])
            wo = w_o_d[l].rearrange("(kc p) (n j) -> p n kc j", p=128, j=128)
            for i in range(2):
                prep(l, "wo%d" % i, [(ck(c * 1024, 8), wo[:, 4 * i + c]) for c in range(4)])
            wup = w_up_d[l].rearrange("(kc p) (c j) -> p c kc j", p=128, j=128)
            for i in range(11):
                prep(l, "up%d" % i, [(ck(0, 8), wup[:, 2 * i]), (ck(1024, 8), wup[:, 22 + 2 * i]),
                                     (ck(2048, 8), wup[:, 2 * i + 1]), (ck(3072, 8), wup[:, 22 + 2 * i + 1])])
            wdn = w_dn_d[l].rearrange("(kc p) (n j) -> p n kc j", p=128, j=128)
            for n in range(8):
                prep(l, "dn%d" % n, [(ck(0, 22), wdn[:, n])])
        P.barrier()

        wseq = [(l, gi) for _t in range(NT) for l in layers for gi in range(len(GROUPS))]
        wst = {"issued": 0, "next": 0}

        def w_issue():
            i = wst["issued"]
            l, gi = wseq[i]
            _, off, ncols = GROUPS[gi]
            s = i % NSLOT
            P.dma("sp", wslot[s][:, 0:ncols], wscr[l, :, off:off + ncols], "wld%d" % s,
                  writes=[("wslot", s)])
            wst["issued"] += 1

        def w_acquire(expect, keep=0):
            i = wst["next"]
            assert GROUPS[wseq[i][1]][0] == expect, (GROUPS[wseq[i][1]][0], expect)
            while wst["issued"] < min(len(wseq), i + NSLOT - keep):
                w_issue()
            wst["next"] += 1
            s = i % NSLOT
            return wslot[s], ("wslot", s)

        sq_ctr = [0]
        SS = 7

        def add_sq(src, first, last):
            b = sq[sq_ctr[0] % 2]
            k = ("sq", sq_ctr[0] % 2)
            sq_ctr[0] += 1
            P.act(V(b[:], k), src, AF.Square)
            P.mm(PS(SS), V(onesb[:], "onesb"), V(b[:], k), first, last)

        def finish_rstd():
            P.act(V(rr[:], "rr"), PS(SS), AF.Sqrt, bias=sm_eps)
            P.recip(PS(SS), V(rr[:], "rr"))

        epsb = sb("epsb", [128, 1], F32)
        P.memset("pool", V(epsb[:], "epsb"), EPS)
        sm_eps = V(epsb[:], "epsb")

        def xTv(c, lo=0, hi=T):
            return V(xT[:, c, lo:hi], ("xT", c))

        def hv(c, lo=0, hi=T):
            return V(h[:, c, lo:hi], ("h", c))

        def pre_norm(gname, l):
            for c in range(NCH):
                add_sq(xTv(c), c == 0, c == NCH - 1)
            finish_rstd()
            for c in range(NCH):
                P.stt(hv(c), xTv(c), sm(gname, l * 8 + c), PS(SS), ALU.mult, ALU.mult)

        tmp_ctr = [0]

        def post_norm_update():
            for n in range(NCH):
                tb_ = tmpn[tmp_ctr[0] % 2]
                k = ("tmpn", tmp_ctr[0] % 2)
                tmp_ctr[0] += 1
                P.tt("dve", V(tb_[:], k), r_f32(R3, "R3", 2 * n), PS(SS), ALU.mult)
                P.tt("pool", xTv(n), xTv(n), V(tb_[:], k), ALU.add)

        def layer_tile(l, tile):
            P.dma("sp", biasT[:].rearrange("p a b c -> p (a b c)"), bscr[l], "bld",
                  reads=[("bscr", l)], writes=["biasT"])
            P.stage = "A"
            pre_norm("g_mpre", l)
            P.stage = "B"
            wpg, wpgk = w_acquire("wpg")
            slot, sk = w_acquire("u", keep=1)
            for g in range(4):
                b = bank()
                for kc in range(8):
                    P.mm(PS(b), V(slot[:, (g * 8 + kc) * 128:(g * 8 + kc + 1) * 128], sk), hv(kc), kc == 0, kc == 7)
                Ub = U[g % 2]
                uk = ("U", g % 2)
                P.cp("act", V(Ub[:, 16:16 + T], uk), PS(b))
                P.cp("pool", V(Ub[:, 0:16], uk), V(uhalo[l][:, g, :], ("uhalo", l)))
                P.cp("pool", V(uhalo[l][:, g, :], ("uhalo", l)), V(Ub[:, T:T + 16], uk))
                prev, pk, lo = Ub, uk, 0
                for step in range(g + 1):
                    sh = 1 << step
                    nb = sAB[step % 2]
                    nk = ("sAB", step % 2)
                    lo2 = lo + sh
                    P.tt("pool", V(nb[:, lo2:16 + T], nk), V(prev[:, lo2:16 + T], pk),
                         V(prev[:, lo2 - sh:16 + T - sh], pk), ALU.add)
                    prev, pk, lo = nb, nk, lo2
                w_ = 1 << (g + 1)
                pooled = r_bf(R3, "R3", 4 + g)
                P.stt(pooled, V(prev[:, 16:16 + T], pk), 1.0 / w_, V(Ub[:, 16:16 + T], uk), ALU.mult, ALU.subtract)
                if tile == 0:
                    ic = SM["invc"] + g * 16
                    P.tt("pool", V(t16[:], "t16"), V(prev[:, 16:32], pk), V(smalls[:, ic:ic + 16], "smalls"), ALU.mult)
                    P.tt("pool", V(pooled.ap[:, 0:16], *pooled.keys), V(t16[:], "t16"), V(Ub[:, 16:32], uk), ALU.subtract)
                b2 = bank()
                P.mm(PS(b2), V(wpg[:, g * 128:(g + 1) * 128], wpgk), pooled, True, True)
                P.act(r_bf(R3, "R3", 8 + g), PS(b2), AF.Identity, scale=sm("pscale", l * 4 + g))
            P.memset("pool", V(R3[64:128, 0:2048], *[("R3", i) for i in range(4)]), 0.0)
            P.memset("pool", V(R3[0:64, 2048:4096], *[("R3", 4 + i) for i in range(4)]), 0.0)
            slot, sk = w_acquire("q")
            for n in range(4):
                b = bank()
                for kc in range(8):
                    P.mm(PS(b), V(slot[:, (n * 8 + kc) * 128:(n * 8 + kc + 1) * 128], sk), hv(kc), kc == 0, kc == 7)
                eng = "act" if n % 2 == 0 else "dve"
                P.cp(eng, V(R3[0:64, n * 512:(n + 1) * 512], ("R3", n)), V(ps[0:64, b, :], ("ps", b)))
                P.cp(eng, V(R3[64:128, (4 + n) * 512:(5 + n) * 512], ("R3", 4 + n)), V(ps[64:128, b, :], ("ps", b)))
            slot, sk = w_acquire("k")
            for n in range(4):
                b = bank()
                for kc in range(8):
                    P.mm(PS(b), V(slot[:, (n * 8 + kc) * 128:(n * 8 + kc + 1) * 128], sk), hv(kc), kc == 0, kc == 7)
                P.cp("dve" if n % 2 else "act", V(kT[l][:, n, T:2 * T], ("kT", l)), PS(b))
            slot, sk = w_acquire("v")
            s4 = slot[:, 0:4096].rearrange("p (c kc j) -> p c kc j", c=4, kc=8)
            for tb in range(TB):
                b = bank()
                for kc in range(8):
                    P.mm(PS(b), hv(kc, tb * 128, (tb + 1) * 128), V(s4[:, :, kc, :], sk), kc == 0, kc == 7)
                eng = "act" if tb % 2 == 0 else "dve"
                P.cp(eng, V(Vb[l][:, 4 + tb, :, 0:64], ("Vb", l)),
                     V(ps[:, b, :].rearrange("p (h d) -> p h d", h=8), ("ps", b)))

            P.stage = "C"
            OB = (5, 6)
            units = [(qp, hg) for qp in range(TB) for hg in range(2)]

            def scores(ui):
                qp, hg = units[ui]
                jlo = max(0, 4 - (tile * TB + qp))
                bd = bank(5)
                ptm = {}
                for hh in range(4):
                    hd = hg * 4 + hh
                    n = hd // 2
                    sl = n if hd % 2 == 0 else 4 + n
                    qv = V(R3[:, sl * 512 + qp * 128: sl * 512 + (qp + 1) * 128], ("R3", sl))
                    if jlo < 4:
                        bm = bank(5)
                        for j in range(jlo, 4):
                            kv = V(kT[l][:, n, (qp + j) * 128:(qp + j + 1) * 128], ("kT", l))
                            P.mm(PS(bm, j * 128, (j + 1) * 128), kv, qv, True, False)
                            P.mm(PS(bm, j * 128, (j + 1) * 128), V(identb[:], "identb"),
                                 V(biasT[:, hd, j, :], "biasT"), False, True)
                        ptv = r_bf(R1, "R1", (ui % 2) * 4 + hh)
                        P.act(V(ptv.ap[:, jlo * 128:512], *ptv.keys), PS(bm, jlo * 128, 512), AF.Exp, scale=0.125)
                        ptm[hh] = ptv
                    kv = V(kT[l][:, n, (qp + 4) * 128:(qp + 5) * 128], ("kT", l))
                    P.mm(PS(bd, hh * 128, (hh + 1) * 128), kv, qv, True, False)
                    P.mm(PS(bd, hh * 128, (hh + 1) * 128), V(identb[:], "identb"),
                         V(biasT[:, hd, 4, :], "biasT"), False, True)
                ptd = r_bf(R1, "R1", 8 + ui % 2)
                P.act(ptd, PS(bd), AF.Exp, scale=0.125)
                return ptm, ptd, jlo

            def pv(ui, pts):
                qp, hg = units[ui]
                ptm, ptd, jlo = pts
                for hh in range(4):
                    hd = hg * 4 + hh
                    ov = PS(OB[hg], hh * 65, hh * 65 + 65)
                    for j in range(jlo, 4):
                        P.mm(ov, V(ptm[hh].ap[:, j * 128:(j + 1) * 128], *ptm[hh].keys),
                             V(Vb[l][:, qp + j, hd, :], ("Vb", l)), j == jlo, False)
                    P.mm(ov, V(ptd.ap[:, hh * 128:(hh + 1) * 128], *ptd.keys),
                         V(Vb[l][:, qp + 4, hd, :], ("Vb", l)), jlo == 4, True)

            def normalize(qp, hg):
                attn_tok = r_bf(R1, "R1", 10 + qp % 2)
                o3 = ps[:, OB[hg], 0:260].rearrange("p (h d) -> p h d", h=4)
                P.recip(V(rden[:, hg * 4:hg * 4 + 4], "rden"), V(o3[:, :, 64], ("ps", OB[hg])))
                P.tt("dve", V(attn_tok.ap[:, hg * 256:(hg + 1) * 256].rearrange("p (h d) -> p h d", h=4), *attn_tok.keys),
                     V(o3[:, :, 0:64], ("ps", OB[hg])),
                     V(rden[:, hg * 4:hg * 4 + 4].unsqueeze(2).to_broadcast([128, 4, 64]), "rden"), ALU.mult)

            def transposes(qp):
                attn_tok = r_bf(R1, "R1", 10 + qp % 2)
                bt_ = bank(5)
                psb = ps[:, bt_, :].bitcast(BF16)
                for n in range(4):
                    P.tr(V(psb[:, n * 128:(n + 1) * 128], ("ps", bt_)),
                         V(attn_tok.ap[:, n * 128:(n + 1) * 128], *attn_tok.keys), V(identb[:], "identb"))
                outv = R3[:, 12 * 512:16 * 512].rearrange("p (n t) -> p n t", n=4)[:, :, qp * 128:(qp + 1) * 128]
                P.cp("act", V(outv, *[("R3", 12 + n) for n in range(4)]),
                     V(psb[:, 0:512].rearrange("p (n t) -> p n t", n=4), ("ps", bt_)))

            pend = scores(0)
            deferred = []
            for ui in range(len(units)):
                if not PIPE:
                    if ui > 0:
                        pend = scores(ui)
                    pv(ui, pend)
                    qp, hg = units[ui]
                    normalize(qp, hg)
                    if hg == 1:
                        transposes(qp)
                    continue
                nxt = scores(ui + 1) if ui + 1 < len(units) else None
                pv(ui, pend)
                for f in deferred:
                    f()
                deferred = []
                qp, hg = units[ui]
                normalize(qp, hg)
                if hg == 1:
                    if DEFER:
                        deferred.append(lambda qp=qp: transposes(qp))
                    else:
                        transposes(qp)
                pend = nxt
            for f in deferred:
                f()
            P.cp("pool", V(kT[l][:, :, 0:T], ("kT", l)), V(kT[l][:, :, T:2 * T], ("kT", l)))
            P.cp("pool", V(Vb[l][:, 0:4, :, 0:64], ("Vb", l)), V(Vb[l][:, 4:8, :, 0:64], ("Vb", l)))

            P.stage = "E"
            for n in range(NCH):
                gsl, gsk = w_acquire("e%d" % n)
                bga, bgb, bya, byb = bank(), bank(), bank(), bank()
                for kc in range(8):
                    c0 = kc * 128
                    P.mm(PS(bga), V(gsl[:, c0:c0 + 128], gsk), hv(kc), kc == 0, kc == 7)
                for kc in range(8):
                    c0 = 1024 + kc * 128
                    P.mm(PS(bgb), V(gsl[:, c0:c0 + 128], gsk), hv(kc), kc == 0, kc == 7)
                for kc in range(4):
                    c0 = 2048 + kc * 128
                    P.mm(PS(bya), V(gsl[:, c0:c0 + 128], gsk), r_bf(R3, "R3", 12 + kc), kc == 0, kc == 3)
                for kc in range(4):
                    c0 = 2560 + kc * 128
                    P.mm(PS(byb), V(gsl[:, c0:c0 + 128], gsk), r_bf(R3, "R3", 8 + kc), kc == 0, kc == 3)
                r = n % 2
                siga, sigb = r_f32(R1, "R1", 4 * r), r_f32(R1, "R1", 4 * r + 2)
                t1, t2 = r_f32(R1, "R1", 8 + 4 * r), r_f32(R1, "R1", 10 + 4 * r)
                P.act(siga, PS(bga), AF.Sigmoid, bias=sm("b_gate", l * 16 + n))
                P.act(sigb, PS(bgb), AF.Sigmoid, bias=sm("b_gate", l * 16 + 8 + n))
                P.tt("dve", t1, PS(bya), siga, ALU.mult)
                P.tt("dve", t2, PS(byb), sigb, ALU.mult)
                P.tt("pool", r_bf(R2, "R2", n), t1, t2, ALU.add)
            P.stage = "F"
            for n in range(NCH):
                if n % 4 == 0:
                    slot, sk = w_acquire("wo%d" % (n // 4))
                b = bank()
                for kc in range(8):
                    c0 = ((n % 4) * 8 + kc) * 128
                    P.mm(PS(b), V(slot[:, c0:c0 + 128], sk), r_bf(R2, "R2", kc), kc == 0, kc == 7)
                P.act(r_f32(R3, "R3", 2 * n), PS(b), AF.Identity, scale=sm("g_mpost", l * 8 + n))
                add_sq(PS(b), n == 0, n == NCH - 1)
            finish_rstd()
            post_norm_update()
            P.stage = "G"
            pre_norm("g_fpre", l)
            P.stage = "H"
            cw = lambda i, m: sm("convw", l * 132 + i * 44 + m)
            cb = lambda m: sm("convb", l * 44 + m)
            cv_ctr = [0]

            def conv_old(b, m, dst):
                P.act(dst, PS(b), AF.Identity, bias=cb(m), scale=cw(2, m))
                d1 = V(dst.ap[:, 1:T], *dst.keys)
                P.stt(d1, PS(b, 0, T - 1), cw(1, m), d1, ALU.mult, ALU.add)
                d2 = V(dst.ap[:, 2:T], *dst.keys)
                P.stt(d2, PS(b, 0, T - 2), cw(0, m), d2, ALU.mult, ALU.add)
                hk = ("chalo", l, m)
                d02 = V(dst.ap[:, 0:2], *dst.keys)
                P.stt(d02, V(chalo[l][:, m, :], hk), cw(0, m), d02, ALU.mult, ALU.add)
                d01 = V(dst.ap[:, 0:1], *dst.keys)
                P.stt(d01, V(chalo[l][:, m, 1:2], hk), cw(1, m), d01, ALU.mult, ALU.add)
                P.cp("act", V(chalo[l][:, m, :], hk), PS(b, T - 2, T))

            for j in range(11 if not HNEW else 0):
                slot, sk = w_acquire("up%d" % j)
                for a in range(2):
                    i = 2 * j + a
                    bv, bg = bank(), bank()
                    for kc in range(8):
                        c0 = ((2 * a) * 8 + kc) * 128
                        P.mm(PS(bv), V(slot[:, c0:c0 + 128], sk), hv(kc), kc == 0, kc == 7)
                    for kc in range(8):
                        c0 = ((2 * a + 1) * 8 + kc) * 128
                        P.mm(PS(bg), V(slot[:, c0:c0 + 128], sk), hv(kc), kc == 0, kc == 7)
                    r = cv_ctr[0] % 2
                    cv_ctr[0] += 1
                    aval, agate, gg = r_f32(R1, "R1", 2 * r), r_f32(R1, "R1", 4 + 2 * r), r_f32(R1, "R1", 8 + 2 * r)
                    conv_old(bv, i, aval)
                    conv_old(bg, 22 + i, agate)
                    P.act(gg, agate, AF.Gelu_apprx_tanh)
                    P.tt("pool", r_bf(R2, "R2", i), gg, aval, ALU.mult)
            if HNEW:
                par = tile % 2
                cur, prv = cht[(l, par)], cht[(l, 1 - par)]
                ckey = lambda p_, m: ("cht", l, p_, m)

                def conv(b, m, dst):
                    P.act(V(cur[:, m, 0, :], ckey(par, m)), PS(b, 0, 2), AF.Identity)
                    P.act(V(cur[:, m, 1, :], ckey(par, m)), PS(b, T - 2, T), AF.Identity)
                    P.act(dst, PS(b), AF.Identity, bias=cb(m), scale=cw(2, m))
                    d1 = V(dst.ap[:, 1:T], *dst.keys)
                    P.stt(d1, PS(b, 0, T - 1), cw(1, m), d1, ALU.mult, ALU.add)
                    d2 = V(dst.ap[:, 2:T], *dst.keys)
                    P.stt(d2, PS(b, 0, T - 2), cw(0, m), d2, ALU.mult, ALU.add)

                pending = None
                for j in range(11):
                    slot, sk = w_acquire("up%d" % j)
                    for a in range(2):
                        i = 2 * j + a
                        bv, bg = bank(), bank()
                        for kc in range(8):
                            c0 = ((2 * a) * 8 + kc) * 128
                            P.mm(PS(bv), V(slot[:, c0:c0 + 128], sk), hv(kc), kc == 0, kc == 7)
                        for kc in range(8):
                            c0 = ((2 * a + 1) * 8 + kc) * 128
                            P.mm(PS(bg), V(slot[:, c0:c0 + 128], sk), hv(kc), kc == 0, kc == 7)
                        r = i % 4
                        aval, agate = r_f32(R1, "R1", 4 * r), r_f32(R1, "R1", 4 * r + 2)
                        conv(bv, i, aval)
                        conv(bg, 22 + i, agate)
                        if pending is not None:
                            pending()

                        def fin(i=i, aval=aval, agate=agate):
                            P.act(agate, agate, AF.Gelu_apprx_tanh)
                            P.tt("pool", r_bf(R2, "R2", i), agate, aval, ALU.mult)
                        pending = fin
                pending()
                allk = lambda p_: [ckey(p_, m) for m in range(44)]
                c0w = SM["convw"] + l * 132
                wb = lambda i: V(smalls[:, c0w + i * 44:c0w + (i + 1) * 44].unsqueeze(2).to_broadcast([128, 44, 2]), "smalls")
                w1v = V(smalls[:, c0w + 44:c0w + 88], "smalls")
                bb = V(smalls[:, SM["convb"] + l * 44:SM["convb"] + (l + 1) * 44].unsqueeze(2).to_broadcast([128, 44, 2]), "smalls")
                HD = V(cur[:, :, 0, :], *allk(par))
                TL = V(prv[:, :, 1, :], *allk(1 - par))
                fA, fB = V(fixA[:], "fixA"), V(fixB[:], "fixB")
                P.tt("dve", fA, HD, wb(2), ALU.mult)
                P.tt("dve", fA, fA, bb, ALU.add)
                P.tt("dve", fB, TL, wb(0), ALU.mult)
                P.tt("dve", fA, fA, fB, ALU.add)
                P.tt("dve", V(fixB[:, :, 0], "fixB"), V(prv[:, :, 1, 1], *allk(1 - par)), w1v, ALU.mult)
                P.tt("dve", V(fixB[:, :, 1], "fixB"), V(cur[:, :, 0, 0], *allk(par)), w1v, ALU.mult)
                P.tt("dve", fA, fA, fB, ALU.add)
                P.act(V(fixB[:, 22:44, :], "fixB"), V(fixA[:, 22:44, :], "fixA"), AF.Gelu_apprx_tanh)
                r2v = R2[:, :].rearrange("p (i t) -> p i t", i=22)[:, :, 0:2]
                P.tt("dve", V(r2v, *[("R2", i) for i in range(22)]), V(fixB[:, 22:44, :], "fixB"), V(fixA[:, 0:22, :], "fixA"), ALU.mult)

            P.stage = "I"
            for n in range(NCH):
                slot, sk = w_acquire("dn%d" % n)
                b = bank()
                for kc in range(NFF):
                    P.mm(PS(b), V(slot[:, kc * 128:(kc + 1) * 128], sk), r_bf(R2, "R2", kc), kc == 0, kc == NFF - 1)
                P.act(r_f32(R3, "R3", 2 * n), PS(b), AF.Identity, scale=sm("g_fpost", l * 8 + n))
                add_sq(PS(b), n == 0, n == NCH - 1)
            finish_rstd()
            post_norm_update()

        for tile in range(NT):
            t0 = tile * T
            P.stage = "xin"
            for tb in range(TB):
                xi = xio[tb % 2]
                xk = ("xio", tb % 2)
                P.dma("sp", xi[:], x_d[t0 + tb * 128:t0 + (tb + 1) * 128, :], "xld%d" % (tb % 2), writes=[xk])
                for half in range(2):
                    b = bank()
                    for c4 in range(4):
                        c = half * 4 + c4
                        P.tr(PS(b, c4 * 128, (c4 + 1) * 128), V(xi[:, c * 128:(c + 1) * 128], xk), V(identf[:], "identf"))
                    P.cp("dve" if half == 0 else "act",
                         V(xT[:, half * 4:half * 4 + 4, tb * 128:(tb + 1) * 128], *[("xT", half * 4 + c4) for c4 in range(4)]),
                         V(ps[:, b, :].rearrange("p (c t) -> p c t", c=4), ("ps", b)))
            for l in layers:
                layer_tile(l, tile)
            P.stage = "xout"
            for tb in range(TB):
                xo_ = xo[tb % 2]
                xk = ("xo", tb % 2)
                for half in range(2):
                    b = bank()
                    for c4 in range(4):
                        c = half * 4 + c4
                        P.tr(PS(b, c4 * 128, (c4 + 1) * 128), xTv(c, tb * 128, (tb + 1) * 128), V(identf[:], "identf"))
                    P.cp("dve" if half == 0 else "act", V(xo_[:, half * 512:(half + 1) * 512], xk), PS(b))
                P.dma("sp", out_d[t0 + tb * 128:t0 + (tb + 1) * 128, :], xo_[:], "xst%d" % (tb % 2),
                      reads=[xk], writes=["out"])
        P.barrier()
        P.build(st)
    return nc, P


def _host_layout(inputs, layers=(0, 1)):
    f = lambda k: np.asarray(inputs[k], dtype=np.float32)
    sm = np.zeros((128, NSM), np.float32)

    def put(name, arr2d):
        sm[:, SM[name]:SM[name] + arr2d.shape[0]] = arr2d.T
    put("g_mpre", f("norm_mix_pre").reshape(2 * 8, 128))
    put("g_mpost", f("norm_mix_post").reshape(2 * 8, 128))
    put("g_fpre", f("norm_ffn_pre").reshape(2 * 8, 128))
    put("g_fpost", f("norm_ffn_post").reshape(2 * 8, 128))
    put("b_gate", f("b_gate").reshape(2 * 16, 128))
    put("pscale", f("pool_scale").reshape(2 * 4, 128))
    put("convw", f("conv_w").reshape(2 * 3 * 44, 128))
    put("convb", f("conv_b").reshape(2 * 44, 128))
    invc = np.zeros((64, 128), np.float32)
    for g in range(4):
        for t in range(16):
            invc[g * 16 + t, :] = 1.0 / min(t + 1, 1 << (g + 1))
    put("invc", invc)
    m = np.arange(128)[:, None]
    i = np.arange(128)[None, :]
    rb = f("rel_bias")
    btab = np.zeros((2, 128, 8, 5, 128), np.float32)
    maskc = np.zeros((128, 8, 5, 128), np.float32)
    for j in range(5):
        idx = np.clip(128 * (4 - j) + i - m, -256, 256) + 256
        btab[:, :, :, j, :] = np.transpose(rb[:, :, idx], (0, 2, 1, 3))
    NEG = -240000.0
    maskc[0:64, :, 0, 64:128] = NEG
    maskc[64:128, :, 4, 0:64] = NEG
    return sm, btab.reshape(2, 128, 5120), maskc.reshape(128, 5120)


_CACHE = {}


def kernel(**inputs):
    NT = 16
    key = ("prog", NT)
    if key not in _CACHE:
        _CACHE[key] = build_program(NT)[0]
    nc = _CACHE[key]
    sm, btab, maskc = _host_layout(inputs)
    x = np.asarray(inputs["x"], dtype=np.float32)
    shared = {k: np.ascontiguousarray(np.asarray(inputs[k], dtype=np.float32)) for k in
              ["w_in", "w_attn_out", "w_pool_group", "w_pool_out", "w_o", "w_up", "w_down"]}
    shared.update({"btab": btab, "maskc": maskc, "smalls": sm})
    in_maps = []
    for b in range(8):
        m = dict(shared)
        m["x"] = np.ascontiguousarray(x[b])
        in_maps.append(m)
    res = run_bass_kernel_spmd(nc, in_maps, core_ids=list(range(8)))
    return np.stack([np.asarray(r["out"], dtype=np.float32) for r in res.results], axis=0)
```

```python
import numpy as np
from contextlib import ExitStack
import concourse.bass as bass
import concourse.mybir as mybir
from concourse.bass_utils import run_bass_kernel_spmd

F32 = mybir.dt.float32
BF16 = mybir.dt.bfloat16
AF = mybir.ActivationFunctionType
ALU = mybir.AluOpType

ENGS = ("pe", "act", "dve", "pool", "sp")

SEQ = 8192
D = 1024
T = 512
TB = 4
NCH = 8
DFF = 2816
NFF = 22
EPS = 1e-6
NSLOT = 4
import os
PIPE = os.environ.get('K_PIPE', '1') == '1'
DEFER = os.environ.get('K_DEFER', '1') == '1'
HNEW = os.environ.get('K_HNEW', '0') == '1'
HSAVE_DVE = os.environ.get('K_HSD', '1') == '1'
SLOTC = 4096
GROUPS = []
_off = 0
for _nm, _nc in ([("wpg", 512), ("u", 4096), ("q", 4096), ("k", 4096), ("v", 4096)]
                 + [("e%d" % i, 3072) for i in range(8)] + [("wo0", 4096), ("wo1", 4096)]
                 + [("up%d" % i, 4096) for i in range(11)] + [("dn%d" % i, 2816) for i in range(8)]):
    GROUPS.append((_nm, _off, _nc))
    _off += _nc
WCOLS = _off

SM = {}
_o = 0
for _nm, _n in [("g_mpre", 16), ("g_mpost", 16), ("g_fpre", 16), ("g_fpost", 16), ("b_gate", 32),
                ("pscale", 8), ("convw", 2 * 3 * 44), ("convb", 2 * 44), ("invc", 64)]:
    SM[_nm] = _o
    _o += _n
NSM = _o


class V:
    __slots__ = ("ap", "keys")

    def __init__(self, ap, *keys):
        self.ap = ap
        self.keys = tuple(keys)


class Prog:
    def __init__(self, nc):
        self.nc = nc
        self.streams = {e: [] for e in ENGS}
        self.cnt = {e: 0 for e in ENGS}
        self.seen = {e: {} for e in ENGS}
        self.last_w = {}
        self.readers = {}
        self.sems = {}
        self.dma_cnt = {}
        self.n_wait = 0
        self.n_ins = 0
        self.stage = "setup"
        self.stages = {e: [] for e in ENGS}

    def _deps(self, eng, reads, writes, same_ok):
        deps = {}

        def need(tok):
            if tok[2] == eng and same_ok:
                return
            if tok[1] > deps.get(tok[0], 0):
                deps[tok[0]] = tok[1]
        for r in reads:
            lw = self.last_w.get(r)
            if lw is not None:
                need(lw)
            if isinstance(r, tuple) and r[0] == "ps":
                for rd in self.readers.get(r, ()):
                    if rd[2] != eng:
                        need(rd)
        for w in writes:
            lw = self.last_w.get(w)
            if lw is not None:
                need(lw)
            for rd in self.readers.get(w, ()):
                need(rd)
        out = []
        seen = self.seen[eng]
        for k, v in deps.items():
            if seen.get(k, 0) < v:
                seen[k] = v
                out.append((k, v))
        return out

    def _record(self, reads, writes, tok):
        for r in reads:
            lst = self.readers.setdefault(r, [])
            lst[:] = [x for x in lst if x[0] != tok[0]]
            lst.append(tok)
        for w in writes:
            self.last_w[w] = tok
            self.readers[w] = []

    def op(self, eng, fn, reads=(), writes=(), same_ok=False):
        waits = self._deps(eng, reads, writes, same_ok)
        self.cnt[eng] += 1
        tok = (("e", eng), self.cnt[eng], eng)
        self._record(reads, writes, tok)
        self.n_wait += len(waits)
        self.n_ins += 1
        self.stages[eng].append((self.stage, len(waits)))

        def emit(e, waits=waits, fn=fn, eng=eng):
            for k, v in waits:
                e.wait_ge(self.sems[k], v)
            fn(e).then_inc(self.sems[("e", eng)], 1)
        self.streams[eng].append(emit)

    def dma(self, queue, out, in_, semname, reads=(), writes=(), **kw):
        waits = self._deps(queue, reads, writes, False)
        k = ("d", semname)
        self.dma_cnt[k] = self.dma_cnt.get(k, 0) + 16
        tok = (k, self.dma_cnt[k], "dma")
        self._record(reads, writes, tok)
        self.n_wait += len(waits)
        self.n_ins += 1

        def emit(e, waits=waits, k=k):
            for kk, v in waits:
                e.wait_ge(self.sems[kk], v)
            e.dma_start(out=out, in_=in_, **kw).then_inc(self.sems[k], 16)
        self.streams[queue].append(emit)

    def barrier(self):
        targets = [(("e", e), self.cnt[e]) for e in ENGS if self.cnt[e] > 0]
        targets += [(k, v) for k, v in self.dma_cnt.items()]
        for eng in ENGS:
            waits = []
            seen = self.seen[eng]
            for k, v in targets:
                if k == ("e", eng):
                    continue
                if seen.get(k, 0) < v:
                    seen[k] = v
                    waits.append((k, v))

            def emit(e, waits=waits):
                for kk, v in waits:
                    e.wait_ge(self.sems[kk], v)
            self.streams[eng].append(emit)
        self.last_w.clear()
        self.readers.clear()

    @staticmethod
    def _k(*vs):
        ks = []
        for v in vs:
            if isinstance(v, V):
                ks.extend(v.keys)
        return ks

    @staticmethod
    def _a(v):
        return v.ap if isinstance(v, V) else v

    def mm(self, out, lhsT, rhs, start, stop):
        a = self._a
        self.op("pe", lambda e: e.matmul(a(out), a(lhsT), a(rhs), start=start, stop=stop),
                reads=self._k(lhsT, rhs), writes=self._k(out), same_ok=True)

    def tr(self, out, in_, ident):
        a = self._a
        self.op("pe", lambda e: e.transpose(a(out), a(in_), a(ident)),
                reads=self._k(in_, ident), writes=self._k(out), same_ok=True)

    def act(self, out, in_, func, bias=0.0, scale=1.0):
        a = self._a
        self.op("act", lambda e: e.activation(a(out), a(in_), func, bias=a(bias), scale=a(scale)),
                reads=self._k(in_, bias, scale), writes=self._k(out))

    def ts(self, eng, out, in0, s1, s2, op0, op1=None):
        a = self._a
        if op1 is None:
            f = lambda e: e.tensor_scalar(a(out), a(in0), a(s1), None, op0)
        else:
            f = lambda e: e.tensor_scalar(a(out), a(in0), a(s1), a(s2), op0, op1)
        self.op(eng, f, reads=self._k(in0, s1, s2), writes=self._k(out))

    def stt(self, out, in0, s, in1, op0, op1):
        a = self._a
        self.op("dve", lambda e: e.scalar_tensor_tensor(a(out), a(in0), a(s), a(in1), op0, op1),
                reads=self._k(in0, s, in1), writes=self._k(out))

    def tt(self, eng, out, in0, in1, op):
        a = self._a
        self.op(eng, lambda e: e.tensor_tensor(a(out), a(in0), a(in1), op),
                reads=self._k(in0, in1), writes=self._k(out))

    def cp(self, eng, out, in_):
        a = self._a
        if eng == "act":
            self.op(eng, lambda e: e.activation(a(out), a(in_), AF.Identity),
                    reads=self._k(in_), writes=self._k(out))
        else:
            self.op(eng, lambda e: e.tensor_copy(a(out), a(in_)), reads=self._k(in_), writes=self._k(out))

    def memset(self, eng, out, val):
        a = self._a
        self.op(eng, lambda e: e.memset(a(out), val), writes=self._k(out))

    def recip(self, out, in_):
        a = self._a
        self.op("dve", lambda e: e.reciprocal(a(out), a(in_)), reads=self._k(in_), writes=self._k(out))

    def build(self, stack):
        nc = self.nc
        for e in ENGS:
            self.sems[("e", e)] = stack.enter_context(nc.semaphore("s_" + e))
        for k in self.dma_cnt:
            self.sems[k] = stack.enter_context(nc.semaphore("d_" + str(k[1])))
        block = stack.enter_context(nc.Block())
        S = self.streams

        @block.tensor
        def _(e):
            for f in S["pe"]:
                f(e)

        @block.scalar
        def _(e):
            for f in S["act"]:
                f(e)

        @block.vector
        def _(e):
            for f in S["dve"]:
                f(e)

        @block.gpsimd
        def _(e):
            for f in S["pool"]:
                f(e)

        @block.sync
        def _(e):
            for f in S["sp"]:
                f(e)


def build_program(NT=16, layers=(0, 1)):
    nc = bass.Bass("TRN2", target_bir_lowering=False)
    dt_in = lambda name, shape, dt=F32: nc.dram_tensor(name, shape, dt, kind="ExternalInput").ap()
    x_d = dt_in("x", [SEQ, D])
    w_in_d = dt_in("w_in", [2, D, 4096])
    w_ao_d = dt_in("w_attn_out", [2, 512, D])
    w_pg_d = dt_in("w_pool_group", [2, 4, 128, 128])
    w_po_d = dt_in("w_pool_out", [2, 512, D])
    w_o_d = dt_in("w_o", [2, D, D])
    w_up_d = dt_in("w_up", [2, D, 2 * DFF])
    w_dn_d = dt_in("w_down", [2, DFF, D])
    btab_d = dt_in("btab", [2, 128, 5120])
    maskc_d = dt_in("maskc", [128, 5120])
    smalls_d = dt_in("smalls", [128, NSM])
    out_d = nc.dram_tensor("out", [SEQ, D], F32, kind="ExternalOutput").ap()
    wscr = nc.dram_tensor("wscr", [2, 128, WCOLS], BF16, kind="Internal").ap()
    bscr = nc.dram_tensor("bscr", [2, 128, 5120], BF16, kind="Internal").ap()

    P = Prog(nc)
    with ExitStack() as st:
        sb = lambda name, shape, dt: st.enter_context(nc.sbuf_tensor(name, shape, dt))
        wslot = [sb("wslot%d" % i, [128, SLOTC], BF16) for i in range(NSLOT)]
        xT = sb("xT", [128, NCH, T], F32)
        xio = [sb("xio%d" % i, [128, D], F32) for i in range(2)]
        xo = [sb("xo%d" % i, [128, D], F32) for i in range(2)]
        h = sb("h", [128, NCH, T], BF16)
        sq = [sb("sq%d" % i, [128, T], BF16) for i in range(2)]
        kT = {l: sb("kT%d" % l, [128, 4, 2 * T], BF16) for l in layers}
        Vb = {l: sb("Vb%d" % l, [128, 8, 8, 65], BF16) for l in layers}
        U = [sb("U%d" % i, [128, 16 + T], F32) for i in range(2)]
        uhalo = {l: sb("uhalo%d" % l, [128, 4, 16], F32) for l in layers}
        sAB = [sb("sAB%d" % i, [128, 16 + T], F32) for i in range(2)]
        R1 = sb("R1", [128, 16 * 512], BF16)
        R2 = sb("R2", [128, 22 * 512], BF16)
        R3 = sb("R3", [128, 16 * 512], BF16)
        tmpn = [sb("tmpn%d" % i, [128, T], F32) for i in range(2)]
        rr = sb("rr", [128, T], F32)
        cht = {(l, p_): sb("cht%d_%d" % (l, p_), [128, 44, 2, 2], F32) for l in layers for p_ in range(2)}
        fixA = sb("fixA", [128, 44, 2], F32)
        chalo = {l: sb("chalo%d" % l, [128, 44, 2], F32) for l in layers}
        fixB = sb("fixB", [128, 44, 2], F32)
        biasT = sb("biasT", [128, 8, 5, 128], BF16)
        smalls = sb("smalls_sb", [128, NSM], F32)
        identf = sb("identf", [128, 128], F32)
        identb = sb("identb", [128, 128], BF16)
        onesb = sb("onesb", [128, 128], BF16)
        rden = sb("rden", [128, 8], F32)
        t16 = sb("t16", [128, 16], F32)
        ps = st.enter_context(nc.psum_tensor("ps", [128, 8, 512], F32))

        def r_bf(R, name, s):
            return V(R[:, s * 512:(s + 1) * 512], (name, s))

        def r_f32(R, name, s):
            return V(R[:, s * 512:(s + 2) * 512].bitcast(F32), (name, s), (name, s + 1))

        def sm(name, col):
            c = SM[name] + col
            return V(smalls[:, c:c + 1], "smalls")

        bank_ctr = [0]

        def bank(n=7):
            b = bank_ctr[0] % n
            bank_ctr[0] += 1
            return b

        def PS(b, lo=0, hi=512):
            return V(ps[:, b, lo:hi], ("ps", b))

        P.dma("sp", smalls[:], smalls_d, "smld", writes=["smalls"])
        P.memset("pool", V(identf[:], "identf"), 1.0)
        P.op("pool", lambda e: e.affine_select(identf[:], identf[:], [[-1, 128]], ALU.is_equal, 0.0,
                                               base=0, channel_multiplier=1),
             reads=["identf"], writes=["identf"])
        P.cp("dve", V(identb[:], "identb"), V(identf[:], "identf"))
        P.memset("dve", V(onesb[:], "onesb"), 1.0 / 1024.0)
        for l in layers:
            P.memset("pool", V(kT[l][:], ("kT", l)), 0.0)
            P.memset("pool", V(Vb[l][:], ("Vb", l)), 0.0)
            P.memset("pool", V(Vb[l][:, :, :, 64:65], ("Vb", l)), 1.0)
            P.memset("pool", V(uhalo[l][:], ("uhalo", l)), 0.0)
            P.memset("pool", V(chalo[l][:], *[("chalo", l, m) for m in range(44)]), 0.0)
            for p_ in range(2):
                P.memset("pool", V(cht[(l, p_)][:], *[("cht", l, p_, m) for m in range(44)]), 0.0)
        stage_f = R2[:, 0:10240].bitcast(F32)
        stage_m = R1[:, 0:8192].bitcast(F32)
        r2keys = [("R2", s) for s in range(22)]
        r1keys = [("R1", s) for s in range(16)]
        r3keys = [("R3", s) for s in range(16)]
        stage_m2 = R3[:, 0:2048].bitcast(F32)
        P.dma("sp", stage_m, maskc_d[:, 0:4096], "mkld", writes=r1keys)
        P.dma("sp", stage_m2, maskc_d[:, 4096:5120], "mkld2", writes=r3keys)
        for l in layers:
            P.dma("sp", stage_f, btab_d[l], "btld", writes=r2keys)
            bt = biasT[:].rearrange("p a b c -> p (a b c)")
            P.stt(V(bt[:, 0:4096], "biasT"), V(stage_f[:, 0:4096], *r2keys), 8.0, V(stage_m, *r1keys),
                  ALU.mult, ALU.add)
            P.stt(V(bt[:, 4096:5120], "biasT"), V(stage_f[:, 4096:5120], *r2keys), 8.0, V(stage_m2, *r3keys),
                  ALU.mult, ALU.add)
            P.dma("sp", bscr[l], bt, "btst", reads=["biasT"], writes=[("bscr", l)])

        prep_i = [0]

        def prep(l, gname, parts):
            s = prep_i[0] % NSLOT
            prep_i[0] += 1
            gi = [g[0] for g in GROUPS].index(gname)
            _, off, ncols = GROUPS[gi]
            for dstf, src in parts:
                P.dma("pool", dstf(wslot[s]), src, "prep%d" % s, writes=[("wslot", s)])
            P.dma("sp", wscr[l, :, off:off + ncols], wslot[s][:, 0:ncols], "prepo%d" % s,
                  reads=[("wslot", s)], writes=[("wscr", l)])

        def v4(ap, c, kc):
            return ap.rearrange("p (c kc j) -> p c kc j", c=c, kc=kc)

        def ck(lo, kc):
            return lambda s: s[:, lo:lo + kc * 128].rearrange("p (kc j) -> p kc j", kc=kc)

        for l in layers:
            win = w_in_d[l].rearrange("(kc p) (c j) -> p c kc j", p=128, j=128)
            for gi_, nm in enumerate(["q", "k", "v", "u"]):
                prep(l, nm, [(ck(c * 1024, 8), win[:, 4 * gi_ + c]) for c in range(4)])
                if nm == "v":
                    prep(l, "wpg", [(lambda s: s[:, 0:512].rearrange("p (g d) -> p g d", g=4),
                                     w_pg_d[l].rearrange("g c d -> c g d"))])
            wao_r = w_ao_d[l].rearrange("(kc p) (n j) -> p n kc j", p=128, j=128)
            wpo_r = w_po_d[l].rearrange("(kc p) (n j) -> p n kc j", p=128, j=128)
            for n in range(8):
                prep(l, "e%d" % n, [(ck(0, 8), win[:, 16 + n]), (ck(1024, 8), win[:, 24 + n]),
                                    (ck(2048, 4), wao_r[:, n]), (ck(2560, 4), wpo_r[:, n])])
            wo = w_o_d[l].rearrange("(kc p) (n j) -> p n kc j", p=128, j=128)
            for i in range(2):
                prep(l, "wo%d" % i, [(ck(c * 1024, 8), wo[:, 4 * i + c]) for c in range(4)])
            wup = w_up_d[l].rearrange("(kc p) (c j) -> p c kc j", p=128, j=128)
            for i in range(11):
                prep(l, "up%d" % i, [(ck(0, 8), wup[:, 2 * i]), (ck(1024, 8), wup[:, 22 + 2 * i]),
                                     (ck(2048, 8), wup[:, 2 * i + 1]), (ck(3072, 8), wup[:, 22 + 2 * i + 1])])
            wdn = w_dn_d[l].rearrange("(kc p) (n j) -> p n kc j", p=128, j=128)
            for n in range(8):
                prep(l, "dn%d" % n, [(ck(0, 22), wdn[:, n])])
        P.barrier()

        wseq = [(l, gi) for _t in range(NT) for l in layers for gi in range(len(GROUPS))]
        wst = {"issued": 0, "next": 0}

        def w_issue():
            i = wst["issued"]
            l, gi = wseq[i]
            _, off, ncols = GROUPS[gi]
            s = i % NSLOT
            P.dma("sp", wslot[s][:, 0:ncols], wscr[l, :, off:off + ncols], "wld%d" % s,
                  writes=[("wslot", s)])
            wst["issued"] += 1

        def w_acquire(expect, keep=0):
            i = wst["next"]
            assert GROUPS[wseq[i][1]][0] == expect, (GROUPS[wseq[i][1]][0], expect)
            while wst["issued"] < min(len(wseq), i + NSLOT - keep):
                w_issue()
            wst["next"] += 1
            s = i % NSLOT
            return wslot[s], ("wslot", s)

        sq_ctr = [0]
        SS = 7

        def add_sq(src, first, last):
            b = sq[sq_ctr[0] % 2]
            k = ("sq", sq_ctr[0] % 2)
            sq_ctr[0] += 1
            P.act(V(b[:], k), src, AF.Square)
            P.mm(PS(SS), V(onesb[:], "onesb"), V(b[:], k), first, last)

        def finish_rstd():
            P.act(V(rr[:], "rr"), PS(SS), AF.Sqrt, bias=sm_eps)
            P.recip(PS(SS), V(rr[:], "rr"))

        epsb = sb("epsb", [128, 1], F32)
        P.memset("pool", V(epsb[:], "epsb"), EPS)
        sm_eps = V(epsb[:], "epsb")

        def xTv(c, lo=0, hi=T):
            return V(xT[:, c, lo:hi], ("xT", c))

        def hv(c, lo=0, hi=T):
            return V(h[:, c, lo:hi], ("h", c))

        def pre_norm(gname, l):
            for c in range(NCH):
                add_sq(xTv(c), c == 0, c == NCH - 1)
            finish_rstd()
            for c in range(NCH):
                P.stt(hv(c), xTv(c), sm(gname, l * 8 + c), PS(SS), ALU.mult, ALU.mult)

        tmp_ctr = [0]

        def post_norm_update():
            for n in range(NCH):
                tb_ = tmpn[tmp_ctr[0] % 2]
                k = ("tmpn", tmp_ctr[0] % 2)
                tmp_ctr[0] += 1
                P.tt("dve", V(tb_[:], k), r_f32(R3, "R3", 2 * n), PS(SS), ALU.mult)
                P.tt("pool", xTv(n), xTv(n), V(tb_[:], k), ALU.add)

        def layer_tile(l, tile):
            P.dma("sp", biasT[:].rearrange("p a b c -> p (a b c)"), bscr[l], "bld",
                  reads=[("bscr", l)], writes=["biasT"])
            P.stage = "A"
            pre_norm("g_mpre", l)
            P.stage = "B"
            wpg, wpgk = w_acquire("wpg")
            slot, sk = w_acquire("u", keep=1)
            for g in range(4):
                b = bank()
                for kc in range(8):
                    P.mm(PS(b), V(slot[:, (g * 8 + kc) * 128:(g * 8 + kc + 1) * 128], sk), hv(kc), kc == 0, kc == 7)
                Ub = U[g % 2]
                uk = ("U", g % 2)
                P.cp("act", V(Ub[:, 16:16 + T], uk), PS(b))
                P.cp("pool", V(Ub[:, 0:16], uk), V(uhalo[l][:, g, :], ("uhalo", l)))
                P.cp("pool", V(uhalo[l][:, g, :], ("uhalo", l)), V(Ub[:, T:T + 16], uk))
                prev, pk, lo = Ub, uk, 0
                for step in range(g + 1):
                    sh = 1 << step
                    nb = sAB[step % 2]
                    nk = ("sAB", step % 2)
                    lo2 = lo + sh
                    P.tt("pool", V(nb[:, lo2:16 + T], nk), V(prev[:, lo2:16 + T], pk),
                         V(prev[:, lo2 - sh:16 + T - sh], pk), ALU.add)
                    prev, pk, lo = nb, nk, lo2
                w_ = 1 << (g + 1)
                pooled = r_bf(R3, "R3", 4 + g)
                P.stt(pooled, V(prev[:, 16:16 + T], pk), 1.0 / w_, V(Ub[:, 16:16 + T], uk), ALU.mult, ALU.subtract)
                if tile == 0:
                    ic = SM["invc"] + g * 16
                    P.tt("pool", V(t16[:], "t16"), V(prev[:, 16:32], pk), V(smalls[:, ic:ic + 16], "smalls"), ALU.mult)
                    P.tt("pool", V(pooled.ap[:, 0:16], *pooled.keys), V(t16[:], "t16"), V(Ub[:, 16:32], uk), ALU.subtract)
                b2 = bank()
                P.mm(PS(b2), V(wpg[:, g * 128:(g + 1) * 128], wpgk), pooled, True, True)
                P.act(r_bf(R3, "R3", 8 + g), PS(b2), AF.Identity, scale=sm("pscale", l * 4 + g))
            P.memset("pool", V(R3[64:128, 0:2048], *[("R3", i) for i in range(4)]), 0.0)
            P.memset("pool", V(R3[0:64, 2048:4096], *[("R3", 4 + i) for i in range(4)]), 0.0)
            slot, sk = w_acquire("q")
            for n in range(4):
                b = bank()
                for kc in range(8):
                    P.mm(PS(b), V(slot[:, (n * 8 + kc) * 128:(n * 8 + kc + 1) * 128], sk), hv(kc), kc == 0, kc == 7)
                eng = "act" if n % 2 == 0 else "dve"
                P.cp(eng, V(R3[0:64, n * 512:(n + 1) * 512], ("R3", n)), V(ps[0:64, b, :], ("ps", b)))
                P.cp(eng, V(R3[64:128, (4 + n) * 512:(5 + n) * 512], ("R3", 4 + n)), V(ps[64:128, b, :], ("ps", b)))
            slot, sk = w_acquire("k")
            for n in range(4):
                b = bank()
                for kc in range(8):
                    P.mm(PS(b), V(slot[:, (n * 8 + kc) * 128:(n * 8 + kc + 1) * 128], sk), hv(kc), kc == 0, kc == 7)
                P.cp("dve" if n % 2 else "act", V(kT[l][:, n, T:2 * T], ("kT", l)), PS(b))
            slot, sk = w_acquire("v")
            s4 = slot[:, 0:4096].rearrange("p (c kc j) -> p c kc j", c=4, kc=8)
            for tb in range(TB):
                b = bank()
                for kc in range(8):
                    P.mm(PS(b), hv(kc, tb * 128, (tb + 1) * 128), V(s4[:, :, kc, :], sk), kc == 0, kc == 7)
                eng = "act" if tb % 2 == 0 else "dve"
                P.cp(eng, V(Vb[l][:, 4 + tb, :, 0:64], ("Vb", l)),
                     V(ps[:, b, :].rearrange("p (h d) -> p h d", h=8), ("ps", b)))

            P.stage = "C"
            OB = (5, 6)
            units = [(qp, hg) for qp in range(TB) for hg in range(2)]

            def scores(ui):
                qp, hg = units[ui]
                jlo = max(0, 4 - (tile * TB + qp))
                bd = bank(5)
                ptm = {}
                for hh in range(4):
                    hd = hg * 4 + hh
                    n = hd // 2
                    sl = n if hd % 2 == 0 else 4 + n
                    qv = V(R3[:, sl * 512 + qp * 128: sl * 512 + (qp + 1) * 128], ("R3", sl))
                    if jlo < 4:
                        bm = bank(5)
                        for j in range(jlo, 4):
                            kv = V(kT[l][:, n, (qp + j) * 128:(qp + j + 1) * 128], ("kT", l))
                            P.mm(PS(bm, j * 128, (j + 1) * 128), kv, qv, True, False)
                            P.mm(PS(bm, j * 128, (j + 1) * 128), V(identb[:], "identb"),
                                 V(biasT[:, hd, j, :], "biasT"), False, True)
                        ptv = r_bf(R1, "R1", (ui % 2) * 4 + hh)
                        P.act(V(ptv.ap[:, jlo * 128:512], *ptv.keys), PS(bm, jlo * 128, 512), AF.Exp, scale=0.125)
                        ptm[hh] = ptv
                    kv = V(kT[l][:, n, (qp + 4) * 128:(qp + 5) * 128], ("kT", l))
                    P.mm(PS(bd, hh * 128, (hh + 1) * 128), kv, qv, True, False)
                    P.mm(PS(bd, hh * 128, (hh + 1) * 128), V(identb[:], "identb"),
                         V(biasT[:, hd, 4, :], "biasT"), False, True)
                ptd = r_bf(R1, "R1", 8 + ui % 2)
                P.act(ptd, PS(bd), AF.Exp, scale=0.125)
                return ptm, ptd, jlo

            def pv(ui, pts):
                qp, hg = units[ui]
                ptm, ptd, jlo = pts
                for hh in range(4):
                    hd = hg * 4 + hh
                    ov = PS(OB[hg], hh * 65, hh * 65 + 65)
                    for j in range(jlo, 4):
                        P.mm(ov, V(ptm[hh].ap[:, j * 128:(j + 1) * 128], *ptm[hh].keys),
                             V(Vb[l][:, qp + j, hd, :], ("Vb", l)), j == jlo, False)
                    P.mm(ov, V(ptd.ap[:, hh * 128:(hh + 1) * 128], *ptd.keys),
                         V(Vb[l][:, qp + 4, hd, :], ("Vb", l)), jlo == 4, True)

            def normalize(qp, hg):
                attn_tok = r_bf(R1, "R1", 10 + qp % 2)
                o3 = ps[:, OB[hg], 0:260].rearrange("p (h d) -> p h d", h=4)
                P.recip(V(rden[:, hg * 4:hg * 4 + 4], "rden"), V(o3[:, :, 64], ("ps", OB[hg])))
                P.tt("dve", V(attn_tok.ap[:, hg * 256:(hg + 1) * 256].rearrange("p (h d) -> p h d", h=4), *attn_tok.keys),
                     V(o3[:, :, 0:64], ("ps", OB[hg])),
                     V(rden[:, hg * 4:hg * 4 + 4].unsqueeze(2).to_broadcast([128, 4, 64]), "rden"), ALU.mult)

            def transposes(qp):
                attn_tok = r_bf(R1, "R1", 10 + qp % 2)
                bt_ = bank(5)
                psb = ps[:, bt_, :].bitcast(BF16)
                for n in range(4):
                    P.tr(V(psb[:, n * 128:(n + 1) * 128], ("ps", bt_)),
                         V(attn_tok.ap[:, n * 128:(n + 1) * 128], *attn_tok.keys), V(identb[:], "identb"))
                outv = R3[:, 12 * 512:16 * 512].rearrange("p (n t) -> p n t", n=4)[:, :, qp * 128:(qp + 1) * 128]
                P.cp("act", V(outv, *[("R3", 12 + n) for n in range(4)]),
                     V(psb[:, 0:512].rearrange("p (n t) -> p n t", n=4), ("ps", bt_)))

            pend = scores(0)
            deferred = []
            for ui in range(len(units)):
                if not PIPE:
                    if ui > 0:
                        pend = scores(ui)
                    pv(ui, pend)
                    qp, hg = units[ui]
                    normalize(qp, hg)
                    if hg == 1:
                        transposes(qp)
                    continue
                nxt = scores(ui + 1) if ui + 1 < len(units) else None
                pv(ui, pend)
                for f in deferred:
                    f()
                deferred = []
                qp, hg = units[ui]
                normalize(qp, hg)
                if hg == 1:
                    if DEFER:
                        deferred.append(lambda qp=qp: transposes(qp))
                    else:
                        transposes(qp)
                pend = nxt
            for f in deferred:
                f()
            P.cp("pool", V(kT[l][:, :, 0:T], ("kT", l)), V(kT[l][:, :, T:2 * T], ("kT", l)))
            P.cp("pool", V(Vb[l][:, 0:4, :, 0:64], ("Vb", l)), V(Vb[l][:, 4:8, :, 0:64], ("Vb", l)))

            P.stage = "E"
            for n in range(NCH):
                gsl, gsk = w_acquire("e%d" % n)
                bga, bgb, bya, byb = bank(), bank(), bank(), bank()
                for kc in range(8):
                    c0 = kc * 128
                    P.mm(PS(bga), V(gsl[:, c0:c0 + 128], gsk), hv(kc), kc == 0, kc == 7)
                for kc in range(8):
                    c0 = 1024 + kc * 128
                    P.mm(PS(bgb), V(gsl[:, c0:c0 + 128], gsk), hv(kc), kc == 0, kc == 7)
                for kc in range(4):
                    c0 = 2048 + kc * 128
                    P.mm(PS(bya), V(gsl[:, c0:c0 + 128], gsk), r_bf(R3, "R3", 12 + kc), kc == 0, kc == 3)
                for kc in range(4):
                    c0 = 2560 + kc * 128
                    P.mm(PS(byb), V(gsl[:, c0:c0 + 128], gsk), r_bf(R3, "R3", 8 + kc), kc == 0, kc == 3)
                r = n % 2
                siga, sigb = r_f32(R1, "R1", 4 * r), r_f32(R1, "R1", 4 * r + 2)
                t1, t2 = r_f32(R1, "R1", 8 + 4 * r), r_f32(R1, "R1", 10 + 4 * r)
                P.act(siga, PS(bga), AF.Sigmoid, bias=sm("b_gate", l * 16 + n))
                P.act(sigb, PS(bgb), AF.Sigmoid, bias=sm("b_gate", l * 16 + 8 + n))
                P.tt("dve", t1, PS(bya), siga, ALU.mult)
                P.tt("dve", t2, PS(byb), sigb, ALU.mult)
                P.tt("pool", r_bf(R2, "R2", n), t1, t2, ALU.add)
            P.stage = "F"
            for n in range(NCH):
                if n % 4 == 0:
                    slot, sk = w_acquire("wo%d" % (n // 4))
                b = bank()
                for kc in range(8):
                    c0 = ((n % 4) * 8 + kc) * 128
                    P.mm(PS(b), V(slot[:, c0:c0 + 128], sk), r_bf(R2, "R2", kc), kc == 0, kc == 7)
                P.act(r_f32(R3, "R3", 2 * n), PS(b), AF.Identity, scale=sm("g_mpost", l * 8 + n))
                add_sq(PS(b), n == 0, n == NCH - 1)
            finish_rstd()
            post_norm_update()
            P.stage = "G"
            pre_norm("g_fpre", l)
            P.stage = "H"
            cw = lambda i, m: sm("convw", l * 132 + i * 44 + m)
            cb = lambda m: sm("convb", l * 44 + m)
            cv_ctr = [0]

            def conv_old(b, m, dst):
                P.act(dst, PS(b), AF.Identity, bias=cb(m), scale=cw(2, m))
                d1 = V(dst.ap[:, 1:T], *dst.keys)
                P.stt(d1, PS(b, 0, T - 1), cw(1, m), d1, ALU.mult, ALU.add)
                d2 = V(dst.ap[:, 2:T], *dst.keys)
                P.stt(d2, PS(b, 0, T - 2), cw(0, m), d2, ALU.mult, ALU.add)
                hk = ("chalo", l, m)
                d02 = V(dst.ap[:, 0:2], *dst.keys)
                P.stt(d02, V(chalo[l][:, m, :], hk), cw(0, m), d02, ALU.mult, ALU.add)
                d01 = V(dst.ap[:, 0:1], *dst.keys)
                P.stt(d01, V(chalo[l][:, m, 1:2], hk), cw(1, m), d01, ALU.mult, ALU.add)
                P.cp("dve" if HSAVE_DVE else "act", V(chalo[l][:, m, :], hk), PS(b, T - 2, T))

            pend_old = None
            for j in range(11 if not HNEW else 0):
                slot, sk = w_acquire("up%d" % j)
                for a in range(2):
                    i = 2 * j + a
                    bv, bg = bank(), bank()
                    for kc in range(8):
                        c0 = ((2 * a) * 8 + kc) * 128
                        P.mm(PS(bv), V(slot[:, c0:c0 + 128], sk), hv(kc), kc == 0, kc == 7)
                    for kc in range(8):
                        c0 = ((2 * a + 1) * 8 + kc) * 128
                        P.mm(PS(bg), V(slot[:, c0:c0 + 128], sk), hv(kc), kc == 0, kc == 7)
                    r = cv_ctr[0] % 2
                    cv_ctr[0] += 1
                    aval, agate, gg = r_f32(R1, "R1", 2 * r), r_f32(R1, "R1", 4 + 2 * r), r_f32(R1, "R1", 8 + 2 * r)
                    conv_old(bv, i, aval)
                    conv_old(bg, 22 + i, agate)

                    def fin_old(i=i, aval=aval, agate=agate, gg=gg):
                        P.act(gg, agate, AF.Gelu_apprx_tanh)
                        P.tt("pool", r_bf(R2, "R2", i), gg, aval, ALU.mult)
                    if HSAVE_DVE:
                        if pend_old is not None:
                            pend_old()
                        pend_old = fin_old
                    else:
                        fin_old()
            if pend_old is not None:
                pend_old()
            if HNEW:
                par = tile % 2
                cur, prv = cht[(l, par)], cht[(l, 1 - par)]
                ckey = lambda p_, m: ("cht", l, p_, m)

                def conv(b, m, dst):
                    P.act(V(cur[:, m, 0, :], ckey(par, m)), PS(b, 0, 2), AF.Identity)
                    P.act(V(cur[:, m, 1, :], ckey(par, m)), PS(b, T - 2, T), AF.Identity)
                    P.act(dst, PS(b), AF.Identity, bias=cb(m), scale=cw(2, m))
                    d1 = V(dst.ap[:, 1:T], *dst.keys)
                    P.stt(d1, PS(b, 0, T - 1), cw(1, m), d1, ALU.mult, ALU.add)
                    d2 = V(dst.ap[:, 2:T], *dst.keys)
                    P.stt(d2, PS(b, 0, T - 2), cw(0, m), d2, ALU.mult, ALU.add)

                pending = None
                for j in range(11):
                    slot, sk = w_acquire("up%d" % j)
                    for a in range(2):
                        i = 2 * j + a
                        bv, bg = bank(), bank()
                        for kc in range(8):
                            c0 = ((2 * a) * 8 + kc) * 128
                            P.mm(PS(bv), V(slot[:, c0:c0 + 128], sk), hv(kc), kc == 0, kc == 7)
                        for kc in range(8):
                            c0 = ((2 * a + 1) * 8 + kc) * 128
                            P.mm(PS(bg), V(slot[:, c0:c0 + 128], sk), hv(kc), kc == 0, kc == 7)
                        r = i % 4
                        aval, agate = r_f32(R1, "R1", 4 * r), r_f32(R1, "R1", 4 * r + 2)
                        conv(bv, i, aval)
                        conv(bg, 22 + i, agate)
                        if pending is not None:
                            pending()

                        def fin(i=i, aval=aval, agate=agate):
                            P.act(agate, agate, AF.Gelu_apprx_tanh)
                            P.tt("pool", r_bf(R2, "R2", i), agate, aval, ALU.mult)
                        pending = fin
                pending()
                allk = lambda p_: [ckey(p_, m) for m in range(44)]
                c0w = SM["convw"] + l * 132
                wb = lambda i: V(smalls[:, c0w + i * 44:c0w + (i + 1) * 44].unsqueeze(2).to_broadcast([128, 44, 2]), "smalls")
                w1v = V(smalls[:, c0w + 44:c0w + 88], "smalls")
                bb = V(smalls[:, SM["convb"] + l * 44:SM["convb"] + (l + 1) * 44].unsqueeze(2).to_broadcast([128, 44, 2]), "smalls")
                HD = V(cur[:, :, 0, :], *allk(par))
                TL = V(prv[:, :, 1, :], *allk(1 - par))
                fA, fB = V(fixA[:], "fixA"), V(fixB[:], "fixB")
                P.tt("dve", fA, HD, wb(2), ALU.mult)
                P.tt("dve", fA, fA, bb, ALU.add)
                P.tt("dve", fB, TL, wb(0), ALU.mult)
                P.tt("dve", fA, fA, fB, ALU.add)
                P.tt("dve", V(fixB[:, :, 0], "fixB"), V(prv[:, :, 1, 1], *allk(1 - par)), w1v, ALU.mult)
                P.tt("dve", V(fixB[:, :, 1], "fixB"), V(cur[:, :, 0, 0], *allk(par)), w1v, ALU.mult)
                P.tt("dve", fA, fA, fB, ALU.add)
                P.act(V(fixB[:, 22:44, :], "fixB"), V(fixA[:, 22:44, :], "fixA"), AF.Gelu_apprx_tanh)
                r2v = R2[:, :].rearrange("p (i t) -> p i t", i=22)[:, :, 0:2]
                P.tt("dve", V(r2v, *[("R2", i) for i in range(22)]), V(fixB[:, 22:44, :], "fixB"), V(fixA[:, 0:22, :], "fixA"), ALU.mult)

            P.stage = "I"
            for n in range(NCH):
                slot, sk = w_acquire("dn%d" % n)
                b = bank()
                for kc in range(NFF):
                    P.mm(PS(b), V(slot[:, kc * 128:(kc + 1) * 128], sk), r_bf(R2, "R2", kc), kc == 0, kc == NFF - 1)
                P.act(r_f32(R3, "R3", 2 * n), PS(b), AF.Identity, scale=sm("g_fpost", l * 8 + n))
                add_sq(PS(b), n == 0, n == NCH - 1)
            finish_rstd()
            post_norm_update()

        for tile in range(NT):
            t0 = tile * T
            P.stage = "xin"
            for tb in range(TB):
                xi = xio[tb % 2]
                xk = ("xio", tb % 2)
                P.dma("sp", xi[:], x_d[t0 + tb * 128:t0 + (tb + 1) * 128, :], "xld%d" % (tb % 2), writes=[xk])
                for half in range(2):
                    b = bank()
                    for c4 in range(4):
                        c = half * 4 + c4
                        P.tr(PS(b, c4 * 128, (c4 + 1) * 128), V(xi[:, c * 128:(c + 1) * 128], xk), V(identf[:], "identf"))
                    P.cp("dve" if half == 0 else "act",
                         V(xT[:, half * 4:half * 4 + 4, tb * 128:(tb + 1) * 128], *[("xT", half * 4 + c4) for c4 in range(4)]),
                         V(ps[:, b, :].rearrange("p (c t) -> p c t", c=4), ("ps", b)))
            for l in layers:
                layer_tile(l, tile)
            P.stage = "xout"
            for tb in range(TB):
                xo_ = xo[tb % 2]
                xk = ("xo", tb % 2)
                for half in range(2):
                    b = bank()
                    for c4 in range(4):
                        c = half * 4 + c4
                        P.tr(PS(b, c4 * 128, (c4 + 1) * 128), xTv(c, tb * 128, (tb + 1) * 128), V(identf[:], "identf"))
                    P.cp("dve" if half == 0 else "act", V(xo_[:, half * 512:(half + 1) * 512], xk), PS(b))
                P.dma("sp", out_d[t0 + tb * 128:t0 + (tb + 1) * 128, :], xo_[:], "xst%d" % (tb % 2),
                      reads=[xk], writes=["out"])
        P.barrier()
        P.build(st)
    return nc, P


def _host_layout(inputs, layers=(0, 1)):
    f = lambda k: np.asarray(inputs[k], dtype=np.float32)
    sm = np.zeros((128, NSM), np.float32)

    def put(name, arr2d):
        sm[:, SM[name]:SM[name] + arr2d.shape[0]] = arr2d.T
    put("g_mpre", f("norm_mix_pre").reshape(2 * 8, 128))
    put("g_mpost", f("norm_mix_post").reshape(2 * 8, 128))
    put("g_fpre", f("norm_ffn_pre").reshape(2 * 8, 128))
    put("g_fpost", f("norm_ffn_post").reshape(2 * 8, 128))
    put("b_gate", f("b_gate").reshape(2 * 16, 128))
    put("pscale", f("pool_scale").reshape(2 * 4, 128))
    put("convw", f("conv_w").reshape(2 * 3 * 44, 128))
    put("convb", f("conv_b").reshape(2 * 44, 128))
    invc = np.zeros((64, 128), np.float32)
    for g in range(4):
        for t in range(16):
            invc[g * 16 + t, :] = 1.0 / min(t + 1, 1 << (g + 1))
    put("invc", invc)
    m = np.arange(128)[:, None]
    i = np.arange(128)[None, :]
    rb = f("rel_bias")
    btab = np.zeros((2, 128, 8, 5, 128), np.float32)
    maskc = np.zeros((128, 8, 5, 128), np.float32)
    for j in range(5):
        idx = np.clip(128 * (4 - j) + i - m, -256, 256) + 256
        btab[:, :, :, j, :] = np.transpose(rb[:, :, idx], (0, 2, 1, 3))
    NEG = -240000.0
    maskc[0:64, :, 0, 64:128] = NEG
    maskc[64:128, :, 4, 0:64] = NEG
    return sm, btab.reshape(2, 128, 5120), maskc.reshape(128, 5120)


_CACHE = {}


def kernel(**inputs):
    NT = 16
    key = ("prog", NT)
    if key not in _CACHE:
        _CACHE[key] = build_program(NT)[0]
    nc = _CACHE[key]
    sm, btab, maskc = _host_layout(inputs)
    x = np.asarray(inputs["x"], dtype=np.float32)
    shared = {k: np.ascontiguousarray(np.asarray(inputs[k], dtype=np.float32)) for k in
              ["w_in", "w_attn_out", "w_pool_group", "w_pool_out", "w_o", "w_up", "w_down"]}
    shared.update({"btab": btab, "maskc": maskc, "smalls": sm})
    in_maps = []
    for b in range(8):
        m = dict(shared)
        m["x"] = np.ascontiguousarray(x[b])
        in_maps.append(m)
    res = run_bass_kernel_spmd(nc, in_maps, core_ids=list(range(8)))
    return np.stack([np.asarray(r["out"], dtype=np.float32) for r in res.results], axis=0)
```
